# Optimizing a Trainium2 kernel written in Bass

```python
import jax, jax.numpy as jnp
from jax import lax
import numpy as np

D_MODEL = 4096
BATCH = 2
SEQ = 4096
DEPTH = 2

N_MIXERS = 2
N_ATTN_LAYERS = (DEPTH + 1) // 2
N_RNN_LAYERS = DEPTH // 2
DA_HEAD_DIM = 128
DA_HEADS = D_MODEL // (2 * DA_HEAD_DIM)
ROT_DIM = DA_HEAD_DIM // 4
ROPE_THETA = 500000.0
Q_BLOCK = 128
LAMBDA_STD = 0.1
LRU_WIDTH = D_MODEL
LRU_BLOCKS = 16
LRU_BLOCK_W = LRU_WIDTH // LRU_BLOCKS
CONV_WIDTH = 4
CONV_LEFT = 2
LRU_C = 8.0
MEM_TOKENS = 256
X_HEADS = 4
X_HEAD_DIM = 128
X_WIDTH = X_HEADS * X_HEAD_DIM
D_FF = 4 * D_MODEL
EPS = 1e-6

kernel_name = 'hybrid_diffattn_rglru_encoder'


def rmsnorm(x, g):
    xf = x.astype(jnp.float32)
    y = xf * lax.rsqrt(jnp.mean(xf * xf, axis=-1, keepdims=True) + EPS)
    return (y * g.astype(jnp.float32)).astype(x.dtype)


def lambda_init(layer_idx):
    return 0.8 - 0.6 * float(np.exp(-0.3 * layer_idx))


def apply_partial_rope(t, cos, sin):
    half = ROT_DIM // 2
    tf = t.astype(jnp.float32)
    r1, r2, rest = tf[..., :half], tf[..., half:ROT_DIM], tf[..., ROT_DIM:]
    out = jnp.concatenate([r1 * cos - r2 * sin, r2 * cos + r1 * sin, rest], axis=-1)
    return out.astype(t.dtype)


def diff_attention(xn, w_qkv, lq1, lk1, lq2, lk2, subln_g, w_o, cos, sin, lam_init):
    B, S, _ = xn.shape
    q, k, v = jnp.split(xn @ w_qkv, 3, axis=-1)
    q = apply_partial_rope(q.reshape(B, S, DA_HEADS, 2, DA_HEAD_DIM), cos, sin) * (DA_HEAD_DIM ** -0.5)
    k = apply_partial_rope(k.reshape(B, S, DA_HEADS, 2, DA_HEAD_DIM), cos, sin)
    v = v.reshape(B, S, DA_HEADS, 2 * DA_HEAD_DIM)
    f32 = jnp.float32
    lam = (jnp.exp(jnp.sum(lq1.astype(f32) * lk1.astype(f32)))
           - jnp.exp(jnp.sum(lq2.astype(f32) * lk2.astype(f32))) + lam_init)
    nb = S // Q_BLOCK
    qb = q.reshape(B, nb, Q_BLOCK, DA_HEADS, 2, DA_HEAD_DIM).transpose(1, 0, 3, 4, 2, 5)
    kt = k.transpose(0, 2, 3, 1, 4)
    vt = v.transpose(0, 2, 1, 3)

    def block(qblk):
        s = jnp.einsum('bhmqd,bhmkd->bhmqk', qblk, kt).astype(f32)
        p = jax.nn.softmax(s, axis=-1)
        w = p[:, :, 0] - lam * p[:, :, 1]
        return jnp.einsum('bhqk,bhkv->bhqv', w.astype(vt.dtype), vt)

    o = lax.map(block, qb)
    o = o.transpose(1, 0, 3, 2, 4).reshape(B, S, DA_HEADS, 2 * DA_HEAD_DIM)
    o = rmsnorm(o, subln_g) * (1.0 - lam_init)
    return o.reshape(B, S, DA_HEADS * 2 * DA_HEAD_DIM) @ w_o


def _linear_combine(c1, c2):
    a1, b1 = c1
    a2, b2 = c2
    return a1 * a2, a2 * b1 + b2


def _block_diag(u, w, b):
    B, S, _ = u.shape
    ub = u.reshape(B, S, LRU_BLOCKS, LRU_BLOCK_W)
    return (jnp.einsum('bsnc,ncd->bsnd', ub, w) + b).reshape(B, S, LRU_WIDTH)


def rglru_scan(u, w_a, b_a, w_i, b_i, lam, reverse):
    f32 = jnp.float32
    r = jax.nn.sigmoid(_block_diag(u, w_a, b_a).astype(f32))
    ig = jax.nn.sigmoid(_block_diag(u, w_i, b_i).astype(f32))
    log_a = -LRU_C * r * jax.nn.softplus(-lam.astype(f32))
    a = jnp.exp(log_a)
    inp = jnp.sqrt(-jnp.expm1(2.0 * log_a)) * (ig * u.astype(f32))
    _, h = lax.associative_scan(_linear_combine, (a, inp), axis=1, reverse=reverse)
    return h.astype(u.dtype)


def rglru_block(xn, w_in, conv_w, conv_b, wa_f, ba_f, wi_f, bi_f, lam_f,
                wa_b, ba_b, wi_b, bi_b, lam_b, w_out):
    S = xn.shape[1]
    u, gate = jnp.split(xn @ w_in, 2, axis=-1)
    up = jnp.pad(u, ((0, 0), (CONV_LEFT, CONV_WIDTH - 1 - CONV_LEFT), (0, 0)))
    uc = sum((up[:, t:t + S] * conv_w[t] for t in range(CONV_WIDTH)), conv_b)
    h = (rglru_scan(uc, wa_f, ba_f, wi_f, bi_f, lam_f, False)
         + rglru_scan(uc, wa_b, ba_b, wi_b, bi_b, lam_b, True))
    return (h * jax.nn.gelu(gate, approximate=True)) @ w_out


def memory_xattn(xn, mem, mem_g, w_q, w_kv, w_o):
    B, S, _ = xn.shape
    q = (xn @ w_q).reshape(B, S, X_HEADS, X_HEAD_DIM) * (X_HEAD_DIM ** -0.5)
    k, v = jnp.split(rmsnorm(mem, mem_g) @ w_kv, 2, axis=-1)
    k = k.reshape(B, -1, X_HEADS, X_HEAD_DIM)
    v = v.reshape(B, -1, X_HEADS, X_HEAD_DIM)
    p = jax.nn.softmax(jnp.einsum('bqhd,bkhd->bhqk', q, k).astype(jnp.float32), axis=-1)
    o = jnp.einsum('bhqk,bkhd->bqhd', p.astype(v.dtype), v).reshape(B, S, X_WIDTH)
    return o @ w_o


def sq_relu_mlp(xn, w1, w2):
    return jnp.square(jax.nn.relu(xn @ w1)) @ w2


def setup_inputs(seed: int = 0) -> dict:
    key = jax.random.key(seed)
    ks = iter(jax.random.split(key, 64))
    f32 = jnp.float32

    def nrm(shape, scale):
        return jax.random.normal(next(ks), shape, f32) * scale

    def gain(shape):
        return 1.0 + nrm(shape, 0.02)

    def lru_lambda(n):
        a_c = jax.random.uniform(next(ks), (n, LRU_WIDTH), f32, 0.9, 0.999)
        s = a_c ** (1.0 / LRU_C)
        return jnp.log(s) - jnp.log1p(-s)

    NA, NR = N_ATTN_LAYERS, N_RNN_LAYERS
    x = jax.random.normal(next(ks), (BATCH, SEQ, D_MODEL), f32)
    mem = jax.random.normal(next(ks), (BATCH, MEM_TOKENS, D_MODEL), f32)
    offsets = jax.random.randint(next(ks), (BATCH, 1), 0, 1024, jnp.int32)
    positions = jnp.arange(SEQ, dtype=jnp.int32)[None, :] + offsets
    d = D_MODEL ** -0.5
    return {
        'x': x, 'mem': mem, 'positions': positions,
        'attn_norm_g': gain((NA, D_MODEL)),
        'attn_w_qkv': nrm((NA, D_MODEL, 3 * D_MODEL), d),
        'attn_lambda_q1': nrm((NA, DA_HEAD_DIM), LAMBDA_STD),
        'attn_lambda_k1': nrm((NA, DA_HEAD_DIM), LAMBDA_STD),
        'attn_lambda_q2': nrm((NA, DA_HEAD_DIM), LAMBDA_STD),
        'attn_lambda_k2': nrm((NA, DA_HEAD_DIM), LAMBDA_STD),
        'attn_subln_g': gain((NA, 2 * DA_HEAD_DIM)),
        'attn_w_o': nrm((NA, D_MODEL, D_MODEL), d),
        'rnn_norm_g': gain((NR, D_MODEL)),
        'rnn_w_in': nrm((NR, D_MODEL, 2 * LRU_WIDTH), d),
        'rnn_conv_w': nrm((NR, CONV_WIDTH, LRU_WIDTH), CONV_WIDTH ** -0.5),
        'rnn_conv_b': nrm((NR, LRU_WIDTH), 0.01),
        'rnn_wa_f': nrm((NR, LRU_BLOCKS, LRU_BLOCK_W, LRU_BLOCK_W), LRU_BLOCK_W ** -0.5),
        'rnn_ba_f': nrm((NR, LRU_BLOCKS, LRU_BLOCK_W), 0.01),
        'rnn_wi_f': nrm((NR, LRU_BLOCKS, LRU_BLOCK_W, LRU_BLOCK_W), LRU_BLOCK_W ** -0.5),
        'rnn_bi_f': nrm((NR, LRU_BLOCKS, LRU_BLOCK_W), 0.01),
        'rnn_lam_f': lru_lambda(NR),
        'rnn_wa_b': nrm((NR, LRU_BLOCKS, LRU_BLOCK_W, LRU_BLOCK_W), LRU_BLOCK_W ** -0.5),
        'rnn_ba_b': nrm((NR, LRU_BLOCKS, LRU_BLOCK_W), 0.01),
        'rnn_wi_b': nrm((NR, LRU_BLOCKS, LRU_BLOCK_W, LRU_BLOCK_W), LRU_BLOCK_W ** -0.5),
        'rnn_bi_b': nrm((NR, LRU_BLOCKS, LRU_BLOCK_W), 0.01),
        'rnn_lam_b': lru_lambda(NR),
        'rnn_w_out': nrm((NR, LRU_WIDTH, D_MODEL), LRU_WIDTH ** -0.5),
        'xattn_norm_g': gain((DEPTH, D_MODEL)),
        'xattn_mem_g': gain((DEPTH, D_MODEL)),
        'xattn_w_q': nrm((DEPTH, D_MODEL, X_WIDTH), d),
        'xattn_w_kv': nrm((DEPTH, D_MODEL, 2 * X_WIDTH), d),
        'xattn_w_o': nrm((DEPTH, X_WIDTH, D_MODEL), X_WIDTH ** -0.5),
        'mlp_norm_g': gain((DEPTH, D_MODEL)),
        'mlp_w1': nrm((DEPTH, D_MODEL, D_FF), d),
        'mlp_w2': nrm((DEPTH, D_FF, D_MODEL), D_FF ** -0.5),
        'final_g': gain((D_MODEL,)),
    }


def reference(x, mem, positions,
              attn_norm_g, attn_w_qkv, attn_lambda_q1, attn_lambda_k1, attn_lambda_q2,
              attn_lambda_k2, attn_subln_g, attn_w_o,
              rnn_norm_g, rnn_w_in, rnn_conv_w, rnn_conv_b,
              rnn_wa_f, rnn_ba_f, rnn_wi_f, rnn_bi_f, rnn_lam_f,
              rnn_wa_b, rnn_ba_b, rnn_wi_b, rnn_bi_b, rnn_lam_b, rnn_w_out,
              xattn_norm_g, xattn_mem_g, xattn_w_q, xattn_w_kv, xattn_w_o,
              mlp_norm_g, mlp_w1, mlp_w2, final_g):
    inv_freq = ROPE_THETA ** (-jnp.arange(0, ROT_DIM, 2, dtype=jnp.float32) / ROT_DIM)
    ang = positions.astype(jnp.float32)[..., None] * inv_freq
    cos = jnp.cos(ang)[:, :, None, None, :]
    sin = jnp.sin(ang)[:, :, None, None, :]
    h = x
    for i in range(DEPTH):
        j = i // N_MIXERS
        if i % N_MIXERS == 0:
            h = h + diff_attention(rmsnorm(h, attn_norm_g[j]), attn_w_qkv[j],
                                   attn_lambda_q1[j], attn_lambda_k1[j],
                                   attn_lambda_q2[j], attn_lambda_k2[j],
                                   attn_subln_g[j], attn_w_o[j], cos, sin, lambda_init(i))
        else:
            h = h + rglru_block(rmsnorm(h, rnn_norm_g[j]), rnn_w_in[j], rnn_conv_w[j], rnn_conv_b[j],
                                rnn_wa_f[j], rnn_ba_f[j], rnn_wi_f[j], rnn_bi_f[j], rnn_lam_f[j],
                                rnn_wa_b[j], rnn_ba_b[j], rnn_wi_b[j], rnn_bi_b[j], rnn_lam_b[j],
                                rnn_w_out[j])
        h = h + memory_xattn(rmsnorm(h, xattn_norm_g[i]), mem, xattn_mem_g[i],
                             xattn_w_q[i], xattn_w_kv[i], xattn_w_o[i])
        h = h + sq_relu_mlp(rmsnorm(h, mlp_norm_g[i]), mlp_w1[i], mlp_w2[i])
    return rmsnorm(h, final_g)
```

```python
import contextlib
import math
import numpy as np
import concourse.bass as bass
import concourse.mybir as mybir
from concourse.bass_utils import run_bass_kernel_spmd

F32 = mybir.dt.float32
BF16 = mybir.dt.bfloat16
I32 = mybir.dt.int32
ALU = mybir.AluOpType
AF = mybir.ActivationFunctionType
AX = mybir.AxisListType

D = 4096
T = 1024
SEQ = 4096
EPS = 1e-6
NG = 256
DEBUG = {"dump": [], "stop": None, "snap": False}
WNAMES = [("wqkv", 6144), ("wo", 2048), ("win", 4096), ("wout", 2048),
          ("xq0", 256), ("xkv0", 512), ("xo0", 256), ("xq1", 256), ("xkv1", 512), ("xo1", 256),
          ("w1_0", 8192), ("w2_0", 8192), ("w1_1", 8192), ("w2_1", 8192), ("wg", 512)]
WOFF = {}
PIECES = []
_r = 0
for _n, _rows in WNAMES:
    WOFF[_n] = (_r, _rows)
    for _p in range(0, _rows, 2048):
        PIECES.append((_n, _p, min(2048, _rows - _p)))
    _r += _rows
WROWS = _r


class S:
    def __init__(self, h, name):
        self.h = h
        self.n = 0
        self.name = name


class Rec:
    def __init__(self):
        self.calls = []

    def __getattr__(self, name):
        def f(*a, **k):
            self.calls.append((name, a, k))
            return None
        return f


class Prog:
    ENG = ["sync", "gpsimd", "tensor", "scalar", "vector"]

    def __init__(self, nc):
        self.nc = nc
        self.q = {e: [] for e in self.ENG}
        self.stack = contextlib.ExitStack()
        self.sems = []
        self.nbar = 0
        self.s_bar = None

    def sem(self, name):
        h = self.stack.enter_context(self.nc.semaphore(name))
        s = S(h, name)
        self.sems.append(s)
        return s

    def sbuf(self, name, shape, dt):
        return self.stack.enter_context(self.nc.sbuf_tensor(name, shape, dt))

    def psum(self, name, shape, dt):
        return self.stack.enter_context(self.nc.psum_tensor(name, shape, dt))

    def op(self, eng, fn, waits=(), inc=None, dma=False):
        rec = Rec()
        fn(rec)
        calls = rec.calls
        assert len(calls) == 1
        fn = calls[0]
        val = None
        if inc is not None:
            inc.n += 16 if dma else 1
            val = inc.n
            assert val < 65000, inc.name
        self.q[eng].append((fn, tuple(w for w in waits if w is not None), inc, 16 if dma else 1))
        return (inc, val) if inc is not None else None

    def barrier(self):
        cur = [(s, s.n) for s in self.sems if s.n > 0 and s is not self.s_bar]
        self.nbar += 1
        for e in self.ENG:
            self.op(e, lambda eng: eng.sem_inc(self.s_bar.h, 1), waits=cur)
        self.s_bar.n += len(self.ENG)
        tgt = self.s_bar.n
        for e in self.ENG:
            self.q[e].append((None, ((self.s_bar, tgt),), None, 0))

    def emit(self):
        with self.nc.Block() as block:
            for e in self.ENG:
                items = self.q[e]

                def body(eng, items=items):
                    last = {}
                    for fn, waits, inc, n in items:
                        for (s, v) in waits:
                            if v is None or v <= 0:
                                continue
                            if last.get(s.name, 0) >= v:
                                continue
                            eng.wait_ge(s.h, v)
                            last[s.name] = v
                        if fn is None:
                            continue
                        name, a_, k_ = fn
                        ins = getattr(eng, name)(*a_, **k_)
                        if inc is not None:
                            ins.then_inc(inc.h, n)

                getattr(block, e)(body)


def lambda_init(layer_idx):
    return 0.8 - 0.6 * float(np.exp(-0.3 * layer_idx))


class K:
    def __init__(self, nc):
        self.nc = nc
        self.P = Prog(nc)
        self.din = {}
        self.dout = {}
        self.scr = {}

    def inp(self, name, shape, dt=F32):
        t = self.nc.dram_tensor(name, list(shape), dt, kind="ExternalInput")
        self.din[name] = t
        return t.ap()

    def scratch(self, name, shape, dt):
        kind = "ExternalOutput" if name in DEBUG["dump"] else "Internal"
        t = self.nc.dram_tensor(name, list(shape), dt, kind=kind)
        self.scr[name] = t
        return t.ap()


def build():
    nc = bass.Bass("TRN2", target_bir_lowering=False)
    C = K(nc)
    P = C.P
    xseq = C.inp("xseq", [D, SEQ])
    pos_d = C.inp("pos", [1, SEQ], I32)
    memT = C.inp("memT", [D, 256])
    masks_d = C.inp("masks", [128, 24])
    vecs_d = C.inp("vecs", [128, 20 * 32])
    small_d = C.inp("small", [128, 8 + 512 + 32])
    wsh = C.inp("wsh", [WROWS // 8, 8192])
    wsh_i = nc.dram_tensor("wsh_i", [WROWS // 8, 8192], F32).ap()
    wall = {n_: nc.dram_tensor("wall_" + n_, [rows_, 8192], F32).ap() for n_, rows_ in WNAMES}

    def wv(name, cols=8192):
        v = wall[name]
        if cols != 8192:
            v = v.rearrange("r (a c) -> (r a) c", c=cols)
        return v
    wqkv_d, wo_d, win_d, wout_d = wv("wqkv"), wv("wo"), wv("win"), wv("wout")
    xq_d = [wv(f"xq{i}") for i in range(2)]
    xkv_d = [wv(f"xkv{i}") for i in range(2)]
    xo_d = [wv(f"xo{i}", 4 * NG) for i in range(2)]
    w1_d = [wv(f"w1_{i}") for i in range(2)]
    w2_d = [wv(f"w2_{i}") for i in range(2)]
    wg_d = wv("wg", 4 * 2 * 256)
    out_d = nc.dram_tensor("out", [D, T], F32, kind="ExternalOutput").ap()

    h_d = C.scratch("h_d", [D, T], F32)
    Kt_d = C.scratch("Kt_d", [D, SEQ], BF16)
    V_d = C.scratch("V_d", [SEQ, D], BF16)
    Qt_d = C.scratch("Qt_d", [D, T], BF16)
    Op_d = C.scratch("Op_d", [D, T], F32)
    u_d = C.scratch("u_d", [D, T], F32)
    gg_d = C.scratch("gg_d", [D, T], F32)
    ab_d = [C.scratch(f"ab{i}_d", [D, T], F32) for i in range(4)]
    halo_in = C.scratch("halo_in", [128, 96], F32)
    halo_all = C.scratch("halo_all", [8 * 128, 96], F32)
    car_in = C.scratch("car_in", [128, 128], F32)
    car_all = C.scratch("car_all", [8 * 128, 128], F32)

    with P.stack:
        P.s_bar = P.sem("s_bar")
        ARENA = P.sbuf("arena", [128, 57344], BF16)
        Bt = P.sbuf("Bt", [128, 32768], BF16)
        A3 = ARENA[:, 0:32768].rearrange("p (k t) -> p k t", k=32)
        B3 = Bt[:].rearrange("p (k t) -> p k t", k=32)
        Bf = Bt.bitcast(F32)
        WS = [ARENA[:, 32768 + i * 8192: 32768 + (i + 1) * 8192] for i in range(3)]
        ht = [P.sbuf(f"ht{i}", [128, 512], F32) for i in range(4)]
        sq = [P.sbuf(f"sq{i}", [128, 512], BF16) for i in range(2)]
        rstd = P.sbuf("rstd", [128, 1024], F32)
        tmpf = [P.sbuf(f"tmpf{i}", [128, 512], F32) for i in range(2)]
        ob = [P.sbuf(f"ob{i}", [128, 512], BF16) for i in range(2)]
        vecs = P.sbuf("vecs_sb", [128, 20 * 32], F32)
        small = P.sbuf("small_sb", [128, 8 + 512 + 32], F32)
        masks = P.sbuf("masks_sb", [128, 24], F32)
        ones = P.sbuf("ones", [128, 128], BF16)
        epst = P.sbuf("epst", [128, 4], F32)
        scal = P.sbuf("scal", [128, 16], F32)
        carr = P.sbuf("carr", [128, 96], F32)
        ps = [P.psum(f"ps{i}", [128, 512], F32) for i in range(8)]

        s_ld = [P.sem(f"s_ld{i}") for i in range(4)]
        s_st = [P.sem(f"s_st{i}") for i in range(4)]
        s_wld = [P.sem(f"s_wld{i}") for i in range(3)]
        s_mm = P.sem("s_mm")
        s_ev = P.sem("s_ev")
        s_act = P.sem("s_act")
        s_dve = P.sem("s_dve")
        s_pe2 = P.sem("s_pe2")
        s_misc = P.sem("s_misc")
        s_g = P.sem("s_g")
        s_cc = P.sem("s_cc")
        s_x = [P.sem(f"s_x{i}") for i in range(6)]

        st = {"wuse": 0, "wfree": {}, "ld": [0] * 4, "stv": [None] * 4, "wld": [0] * 3,
              "bank": 0, "bankfree": {}, "grp": 0}

        VEC = {n: i for i, n in enumerate(
            ["g_attn", "g_rnn", "g_x0", "g_x1", "g_m0", "g_m1", "g_mlp0", "g_mlp1", "g_fin",
             "cw0", "cw1", "cw2", "cw3", "cb", "lam_f", "lam_b", "ba_f", "bi_f", "ba_b", "bi_b"])}

        def vcol(name, kc):
            i = VEC[name]
            return vecs[:, i * 32 + kc: i * 32 + kc + 1]

        P.op("sync", lambda e: e.dma_start(out=vecs[:], in_=vecs_d), inc=s_misc, dma=True)
        P.op("sync", lambda e: e.dma_start(out=small[:], in_=small_d), inc=s_misc, dma=True)
        P.op("sync", lambda e: e.dma_start(out=masks[:], in_=masks_d), inc=s_misc, dma=True)
        nsh = WROWS // 8
        cpt = []
        for i in range(8):
            a, b = i * nsh // 8, (i + 1) * nsh // 8
            cpt.append(P.op("sync", lambda e, a=a, b=b: e.dma_start(out=wsh_i[a:b, :], in_=wsh[a:b, :]), inc=s_ld[i % 4], dma=True))
        so = 0
        for (wn, r0, n) in PIECES:
            m = n // 8
            P.op("gpsimd", lambda e, so=so, m=m, r0=r0, n=n, wn=wn: e.collective_compute(
                "AllGather", ALU.bypass, replica_groups=[list(range(8))],
                ins=[wsh_i[so:so + m, :].opt()], outs=[wall[wn][r0:r0 + n, :].opt()]),
                waits=cpt, inc=s_cc)
            so += m
        P.op("vector", lambda e: e.memset(ones[:], 1.0), inc=s_dve)
        P.op("vector", lambda e: e.memset(epst[:, 0:1], EPS), inc=s_dve)
        P.op("vector", lambda e: e.memset(epst[:, 1:2], EPS / (0.8 * 0.8)), inc=s_dve)
        P.op("vector", lambda e: e.memset(epst[:, 2:3], -math.pi), inc=s_dve)
        P.op("vector", lambda e: e.memset(epst[:, 3:4], 1.0), inc=s_dve)
        P.barrier()

        def load_tile(slot, src_ap, width=512, extra_waits=()):
            w = list(extra_waits)
            if st["stv"][slot] is not None:
                w.append(st["stv"][slot])
            return P.op("sync", lambda e: e.dma_start(out=ht[slot][:, 0:width], in_=src_ap), waits=w,
                        inc=s_ld[slot], dma=True)

        def store_tile(src_sb_ap, dst_ap, slot, waits):
            tok = P.op("sync", lambda e: e.dma_start(out=dst_ap, in_=src_sb_ap), waits=waits,
                       inc=s_st[slot], dma=True)
            st["stv"][slot] = tok
            return tok

        def norm_phase(src_fn, ntok, gname, dst_fn, ngroups=1, cpg=32, scale=1.0 / D, eps_col=0,
                       store_fn=None):
            TW = min(512, ntok)
            nth = ntok // TW
            for g in range(ngroups):
                gguard = (s_dve, s_dve.n)
                gguard2 = (s_pe2, s_pe2.n)
                use_free = {}
                sq_free = {}
                cnt = 0
                ss_done = []
                for th in range(nth):
                    bank = ps[4 + th]
                    last_pe = None
                    for c in range(cpg):
                        ch = g * cpg + c
                        slot = cnt % 4
                        lt = load_tile(slot, src_fn(ch, th * TW, TW), TW,
                                       extra_waits=[use_free.get(cnt - 4)] + ([gguard] if cnt < 4 else []))
                        at = P.op("scalar", lambda e, slot=slot, k=cnt: e.activation(
                            out=sq[k % 2][:, 0:TW], in_=ht[slot][:, 0:TW], func=AF.Square),
                            waits=[lt, sq_free.get(cnt - 2)] + ([gguard2] if cnt < 2 else []), inc=s_act)
                        use_free[cnt] = at
                        pt = P.op("tensor", lambda e, bank=bank, k=cnt, c=c: e.matmul(
                            bank[:, 0:TW], ones[:], sq[k % 2][:, 0:TW], start=(c == 0), stop=(c == cpg - 1)),
                            waits=[at], inc=s_pe2)
                        sq_free[cnt] = pt
                        last_pe = pt
                        cnt += 1
                    a1 = P.op("scalar", lambda e, bank=bank, th=th: e.activation(
                        out=tmpf[th % 2][:, 0:TW], in_=bank[:, 0:TW], func=AF.Sqrt,
                        bias=epst[:, eps_col:eps_col + 1], scale=scale), waits=[last_pe, gguard], inc=s_act)
                    d1 = P.op("vector", lambda e, th=th: e.reciprocal(
                        out=rstd[:, th * TW:(th + 1) * TW], in_=tmpf[th % 2][:, 0:TW]), waits=[a1, gguard], inc=s_dve)
                    ss_done.append(d1)
                cnt2 = 0
                dfree = {}
                for th in range(nth):
                    for c in range(cpg):
                        ch = g * cpg + c
                        slot = cnt2 % 4
                        ew = [dfree.get(cnt2 - 4)]
                        if cnt2 < 4:
                            ew.append((s_act, s_act.n))
                        lt = load_tile(slot, src_fn(ch, th * TW, TW), TW, extra_waits=ew)
                        gcol = vcol(gname, c) if isinstance(gname, str) else gname(c)
                        if store_fn is None:
                            dt_ = P.op("vector", lambda e, slot=slot, ch=ch, th=th, gcol=gcol: e.scalar_tensor_tensor(
                                out=dst_fn(ch, th * TW, TW), in0=ht[slot][:, 0:TW], scalar=gcol,
                                in1=rstd[:, th * TW:(th + 1) * TW], op0=ALU.mult, op1=ALU.mult),
                                waits=[lt, ss_done[th]], inc=s_dve)
                            dfree[cnt2] = dt_
                        else:
                            dt_ = P.op("vector", lambda e, slot=slot, th=th, gcol=gcol: e.scalar_tensor_tensor(
                                out=ht[slot][:, 0:TW], in0=ht[slot][:, 0:TW], scalar=gcol,
                                in1=rstd[:, th * TW:(th + 1) * TW], op0=ALU.mult, op1=ALU.mult),
                                waits=[lt, ss_done[th]], inc=s_dve)
                            store_tile(ht[slot][:, 0:TW], store_fn(ch, th * TW, TW), slot, [dt_])
                        cnt2 += 1
                P.op("vector", lambda e: e.memset(scal[:, 15:16], 0.0), waits=[(s_dve, s_dve.n)], inc=s_dve)

        def wload(wd, jg, ncols):
            k = st["wuse"]
            slot = k % 3
            w = [st["wfree"].get(k - 3)]
            tok = P.op("gpsimd", lambda e: e.dma_start(out=WS[slot][:, 0:ncols], in_=wd[jg * 128:(jg + 1) * 128, 0:ncols]),
                       waits=w, inc=s_wld[slot], dma=True)
            st["wuse"] += 1
            return k, slot, tok

        def gemm(wd, groups, KC, act_fn, ntok, epilogue, moving=False, prefetch=2):
            ncols = KC * NG
            pend = []
            gi = 0
            loads = {}
            for idx in range(min(prefetch, len(groups))):
                loads[idx] = wload(wd, groups[idx], ncols)
            for idx, jg in enumerate(groups):
                if idx + prefetch < len(groups):
                    loads[idx + prefetch] = wload(wd, groups[idx + prefetch], ncols)
                k, slot, wtok = loads.pop(idx)
                W3 = WS[slot][:, 0:ncols].rearrange("p (k n) -> p k n", k=KC)
                last = None
                if not moving:
                    TW = min(512, ntok)
                    for jl in range(NG // 128):
                        for th in range(ntok // TW):
                            b = st["bank"] % 4
                            bfree = st["bankfree"].get(st["bank"] - 4)
                            bank = ps[b][:, 0:TW]
                            for kc in range(KC):
                                last = P.op("tensor", lambda e, bank=bank, W3=W3, kc=kc, jl=jl, th=th: e.matmul(
                                    bank, W3[:, kc, jl * 128:(jl + 1) * 128], act_fn(kc, th * TW, TW),
                                    start=(kc == 0), stop=(kc == KC - 1)),
                                    waits=([wtok, bfree] if kc == 0 else []),
                                    inc=(s_mm if kc == KC - 1 else None))
                            st["bankfree"][st["bank"]] = epilogue(jg * (NG // 128) + jl, th * TW, TW, bank, last)
                            st["bank"] += 1
                else:
                    for tb in range(ntok // 128):
                        b = st["bank"] % 4
                        bfree = st["bankfree"].get(st["bank"] - 4)
                        bank = ps[b][:, 0:NG]
                        for kc in range(KC):
                            last = P.op("tensor", lambda e, bank=bank, W3=W3, kc=kc, tb=tb: e.matmul(
                                bank, act_fn(kc, tb * 128, 128), W3[:, kc, :],
                                start=(kc == 0), stop=(kc == KC - 1)),
                                waits=([wtok, bfree] if kc == 0 else []),
                                inc=(s_mm if kc == KC - 1 else None))
                        st["bankfree"][st["bank"]] = epilogue(jg, tb * 128, 128, bank, last)
                        st["bank"] += 1
                st["wfree"][k] = last

        rs = {"n": 0}

        def resid_epilogue(src_fn, dst_fn):
            def ep(j, t0, w, bank, mmtok):
                slot = rs["n"] % 4
                rs["n"] += 1
                lt = load_tile(slot, src_fn(j, t0, w), w)
                dt_ = P.op("vector", lambda e: e.tensor_tensor(out=ht[slot][:, 0:w], in0=bank, in1=ht[slot][:, 0:w], op=ALU.add),
                           waits=[mmtok, lt], inc=s_ev)
                store_tile(ht[slot][:, 0:w], dst_fn(j, t0, w), slot, [dt_])
                return dt_
            return ep

        def hd_tile(j, t0, w):
            return h_d[j * 128:(j + 1) * 128, t0:t0 + w]

        actA = lambda kc, t0, w: A3[:, kc, t0:t0 + w]
        actB = lambda kc, t0, w: B3[:, kc, t0:t0 + w]

        def copy_epilogue(dst_fn, scale=1.0, eng="scalar"):
            def ep(j, t0, w, bank, mmtok):
                if eng == "scalar":
                    return P.op("scalar", lambda e: e.activation(out=dst_fn(j, t0, w), in_=bank, func=AF.Identity, scale=scale),
                                waits=[mmtok], inc=s_ev)
                return P.op("vector", lambda e: e.tensor_copy(out=dst_fn(j, t0, w), in_=bank), waits=[mmtok], inc=s_ev)
            return ep

        def xattn_block(i):
            gx = "g_x0" if i == 0 else "g_x1"
            gm = "g_m0" if i == 0 else "g_m1"
            Mn = lambda kc, t0, w: B3[:, kc, t0:t0 + w]
            norm_phase(lambda ch, t0, w: memT[ch * 128:(ch + 1) * 128, t0:t0 + w], 256, gm, Mn)
            P.barrier()
            Kx = lambda hx, t0, w: B3[:, hx, 256 + t0:256 + t0 + w]
            gemm(xkv_d[i], [0, 1], 32, Mn, 256, copy_epilogue(lambda j, t0, w: Kx(j, t0, w)))
            Vx = lambda tb, f0, w: B3[:, 4 + tb, 256 + f0:256 + f0 + w]
            gemm(xkv_d[i], [2, 3], 32, Mn, 256,
                 copy_epilogue(lambda jg, t0, w: Vx(t0 // 128, (jg - 2) * NG, NG), eng="vector"), moving=True)
            P.barrier()
            norm_phase(lambda ch, t0, w: h_d[ch * 128:(ch + 1) * 128, t0:t0 + w], T, gx, actA)
            P.barrier()
            Qx = lambda hx, t0, w: B3[:, 8 + hx, t0:t0 + w]
            gemm(xq_d[i], [0, 1], 32, actA, T, copy_epilogue(lambda j, t0, w: Qx(j, t0, w)))
            P.barrier()
            Ox = lambda hx, t0, w: B3[:, 12 + hx, t0:t0 + w]
            PT = lambda kc: B3[:, 16 + kc, 0:512]
            for hx in range(4):
                for qh in range(2):
                    toks = []
                    for kc in range(2):
                        m1 = P.op("tensor", lambda e, kc=kc: e.matmul(ps[kc][:], Kx(hx, kc * 128, 128), Qx(hx, qh * 512, 512),
                                                                       start=True, stop=True),
                                  waits=[(s_dve, s_dve.n), (s_act, s_act.n)], inc=s_mm)
                        a1 = P.op("scalar", lambda e, kc=kc: e.activation(out=PT(kc), in_=ps[kc][:], func=AF.Exp,
                                                                         scale=128.0 ** -0.5),
                                  waits=[m1, (s_pe2, s_pe2.n)], inc=s_act)
                        toks.append(a1)
                    for kc in range(2):
                        P.op("tensor", lambda e, kc=kc: e.matmul(ps[2][:], ones[:], PT(kc), start=(kc == 0), stop=(kc == 1)),
                             waits=[toks[kc]], inc=s_pe2)
                        p2 = P.op("tensor", lambda e, kc=kc: e.matmul(ps[3][:], Vx(kc, hx * 128, 128), PT(kc),
                                                                       start=(kc == 0), stop=(kc == 1)), inc=s_pe2)
                    d1 = P.op("vector", lambda e: e.reciprocal(out=tmpf[0][:], in_=ps[2][:]), waits=[p2], inc=s_dve)
                    P.op("vector", lambda e: e.tensor_tensor(out=Ox(hx, qh * 512, 512), in0=ps[3][:], in1=tmpf[0][:], op=ALU.mult),
                         waits=[d1], inc=s_dve)
            P.barrier()
            gemm(xo_d[i], list(range(16)), 4, lambda kc, t0, w: Ox(kc, t0, w), T, resid_epilogue(hd_tile, hd_tile))
            P.barrier()

        def mlp_block(i):
            g = "g_mlp0" if i == 0 else "g_mlp1"
            norm_phase(lambda ch, t0, w: h_d[ch * 128:(ch + 1) * 128, t0:t0 + w], T, g, actA)
            P.barrier()
            rr = {"n": 0}

            def relu2_ep(base):
                def ep(j, t0, w, bank, mmtok):
                    k = rr["n"]
                    rr["n"] += 1
                    a1 = P.op("scalar", lambda e: e.activation(out=tmpf[k % 2][:, 0:w], in_=bank, func=AF.Relu),
                              waits=[mmtok, rr.get(k - 2)], inc=s_ev)
                    d1 = P.op("vector", lambda e: e.tensor_tensor(out=B3[:, j - base, t0:t0 + w], in0=tmpf[k % 2][:, 0:w],
                                                                  in1=tmpf[k % 2][:, 0:w], op=ALU.mult),
                              waits=[a1], inc=s_dve)
                    rr[k] = d1
                    return a1
                return ep

            for gq in range(4):
                gemm(w1_d[i], list(range(gq * 16, (gq + 1) * 16)), 32, actA, T, relu2_ep(gq * 32))
                P.barrier()
                gemm(w2_d[i][gq * 16 * 128:(gq + 1) * 16 * 128, :], list(range(16)), 32, actB, T,
                     resid_epilogue(hd_tile, hd_tile))
                P.barrier()

        def attn_layer():
            Ct = Bf[0:32, 0:4096]
            Sn = Bf[0:32, 4096:8192]
            ang = Bf[0:32, 8192:12288]
            posi = Bf.bitcast(I32)[0:32, 12288:16384]
            pt = P.op("sync", lambda e: e.dma_start(out=posi, in_=bass.AP(pos_d.tensor, 0, [[0, 32], [1, SEQ]])), inc=s_misc, dma=True)
            d = P.op("vector", lambda e: e.tensor_copy(out=ang, in_=posi), waits=[pt], inc=s_dve)
            d = P.op("vector", lambda e: e.tensor_scalar(out=ang, in0=ang, scalar1=small[0:32, 2:3], scalar2=None, op0=ALU.mult),
                     waits=[d], inc=s_dve)
            SC = 6.28315
            d1 = P.op("vector", lambda e: e.tensor_scalar(out=Ct, in0=ang, scalar1=1.0 / (2 * math.pi), scalar2=0.25,
                                                          op0=ALU.mult, op1=ALU.add), waits=[d], inc=s_dve)
            d1 = P.op("vector", lambda e: e.tensor_copy(out=posi, in_=Ct), waits=[d1], inc=s_dve)
            d1 = P.op("vector", lambda e: e.tensor_copy(out=Sn, in_=posi), waits=[d1], inc=s_dve)
            d1 = P.op("vector", lambda e: e.tensor_tensor(out=Ct, in0=Ct, in1=Sn, op=ALU.subtract), waits=[d1], inc=s_dve)
            a1 = P.op("scalar", lambda e: e.activation(out=Ct, in_=Ct, func=AF.Sin, scale=SC), waits=[d1], inc=s_act)
            d2 = P.op("vector", lambda e: e.tensor_scalar(out=Sn, in0=ang, scalar1=1.0 / (2 * math.pi), scalar2=None,
                                                          op0=ALU.mult), waits=[d1], inc=s_dve)
            d2 = P.op("vector", lambda e: e.tensor_copy(out=posi, in_=Sn), waits=[d2], inc=s_dve)
            d2 = P.op("vector", lambda e: e.tensor_copy(out=ang, in_=posi), waits=[d2], inc=s_dve)
            d2 = P.op("vector", lambda e: e.tensor_tensor(out=Sn, in0=Sn, in1=ang, op=ALU.subtract), waits=[d2], inc=s_dve)
            a2 = P.op("scalar", lambda e: e.activation(out=Sn, in_=Sn, func=AF.Sin, scale=SC), waits=[d2, a1], inc=s_act)
            P.op("vector", lambda e: e.tensor_scalar(out=Sn, in0=Sn, scalar1=small[0:32, 3:4], scalar2=None, op0=ALU.mult),
                 waits=[a2], inc=s_dve)
            lv = small[:, 8:8 + 512]
            d = P.op("vector", lambda e: e.tensor_tensor(out=tmpf[0][:, 0:128], in0=lv[:, 0:128], in1=lv[:, 128:256], op=ALU.mult),
                     waits=[(s_misc, s_misc.n)], inc=s_dve)
            d = P.op("vector", lambda e: e.tensor_tensor(out=tmpf[0][:, 128:256], in0=lv[:, 256:384], in1=lv[:, 384:512], op=ALU.mult),
                     waits=[d], inc=s_dve)
            d = P.op("vector", lambda e: e.reduce_sum(out=scal[:, 1:2], in_=tmpf[0][:, 0:128], axis=AX.X), waits=[d], inc=s_dve)
            d = P.op("vector", lambda e: e.reduce_sum(out=scal[:, 2:3], in_=tmpf[0][:, 128:256], axis=AX.X), waits=[d], inc=s_dve)
            a = P.op("scalar", lambda e: e.activation(out=scal[:, 3:5], in_=scal[:, 1:3], func=AF.Exp), waits=[d], inc=s_act)
            d = P.op("vector", lambda e: e.tensor_tensor(out=scal[:, 5:6], in0=scal[:, 4:5], in1=scal[:, 3:4], op=ALU.subtract),
                     waits=[a], inc=s_dve)
            P.op("vector", lambda e: e.tensor_scalar(out=scal[:, 0:1], in0=scal[:, 5:6], scalar1=-lambda_init(0), scalar2=None,
                                                     op0=ALU.add), waits=[d], inc=s_dve)
            P.barrier()
            Pm = small[0:32, 520:552]
            qf = [tmpf[0], tmpf[1]]
            qs = {"n": 0, "pend": None}

            def qk_ep(tt):
                def ep(j, t0, w, bank, mmtok):
                    k = qs["n"]
                    qs["n"] += 1
                    tok0 = tt * T + t0
                    if j >= 64:
                        raise AssertionError
                    a1 = P.op("scalar", lambda e: e.activation(out=qf[k % 2][0:32, 0:w], in_=bank[0:32, :], func=AF.Identity),
                              waits=[mmtok, qs.get(("d", k - 2)), qs.get(("st", k - 2))], inc=s_ev)
                    a2 = P.op("scalar", lambda e: e.activation(out=ob[k % 2][:, 0:w], in_=bank, func=AF.Identity),
                              waits=[a1], inc=s_ev)
                    sw = ps[6 + k % 2][0:32, 0:w]
                    p1 = P.op("tensor", lambda e: e.matmul(sw, Pm, qf[k % 2][0:32, 0:w], start=True, stop=True),
                              waits=[a1, qs.get(("d", k - 2))], inc=s_pe2)
                    d1 = P.op("vector", lambda e: e.tensor_tensor(out=ht[2 + k % 2][0:32, 0:w], in0=sw, in1=Sn[:, tok0:tok0 + w], op=ALU.mult),
                              waits=[p1], inc=s_dve)
                    d2 = P.op("vector", lambda e: e.tensor_tensor(out=qf[k % 2][0:32, 0:w], in0=qf[k % 2][0:32, 0:w],
                                                                  in1=Ct[:, tok0:tok0 + w], op=ALU.mult), waits=[d1], inc=s_dve)
                    d3 = P.op("vector", lambda e: e.tensor_tensor(out=ob[k % 2][0:32, 0:w], in0=qf[k % 2][0:32, 0:w],
                                                                  in1=ht[2 + k % 2][0:32, 0:w], op=ALU.add), waits=[d2, a2], inc=s_dve)
                    qs[("d", k)] = d3
                    if j < 32:
                        dst = Qt_d[j * 128:(j + 1) * 128, t0:t0 + w]
                    else:
                        dst = Kt_d[(j - 32) * 128:(j - 31) * 128, tok0:tok0 + w]
                    qs[("st", k)] = P.op("sync", lambda e: e.dma_start(out=dst, in_=ob[k % 2][:, 0:w]), waits=[d3, a2],
                                         inc=s_st[k % 2], dma=True)
                    return a2
                return ep

            vs = {"n": 0}

            def v_ep(tt):
                def ep(jg, t0, w, bank, mmtok):
                    k = vs["n"]
                    vs["n"] += 1
                    f0 = (jg - 32) * NG
                    d1 = P.op("vector", lambda e: e.tensor_copy(out=ob[k % 2][:, 0:NG], in_=bank),
                              waits=[mmtok, vs.get(k - 2)], inc=s_ev)
                    vs[k] = P.op("sync", lambda e: e.dma_start(out=V_d[tt * T + t0: tt * T + t0 + 128, f0:f0 + NG],
                                                               in_=ob[k % 2][:, 0:NG]), waits=[d1], inc=s_st[2 + k % 2], dma=True)
                    return d1
                return ep

            for tt in range(4):
                norm_phase(lambda ch, t0, w, tt=tt: xseq[ch * 128:(ch + 1) * 128, tt * T + t0: tt * T + t0 + w], T, "g_attn", actA)
                P.barrier()
                qs["n"] = 0
                for kk in [k_ for k_ in list(qs.keys()) if isinstance(k_, tuple)]:
                    del qs[kk]
                gemm(wqkv_d, list(range(0 if tt == 0 else 16, 32)), 32, actA, T, qk_ep(tt))
                P.barrier()
                vs_keys = [k_ for k_ in vs if k_ != "n"]
                for kk in vs_keys:
                    del vs[kk]
                vs["n"] = 0
                gemm(wqkv_d, list(range(32, 48)), 32, actA, T, v_ep(tt), moving=True)
                P.barrier()
            if DEBUG["stop"] == "qkv":
                return

            HB = 18432
            def KT(buf, m):
                return ARENA[:, buf * HB + m * 4096: buf * HB + (m + 1) * 4096]
            def VT(buf):
                return ARENA[:, buf * HB + 8192: buf * HB + 16384].rearrange("p (k f) -> p k f", k=32)
            def QT(buf, m):
                return ARENA[:, buf * HB + 16384 + m * 1024: buf * HB + 16384 + (m + 1) * 1024]
            PTb = 2 * HB
            def PTt(g):
                return ARENA[:, PTb + (g % 4) * 512: PTb + (g % 4 + 1) * 512]
            O1 = Bf[:, 0:1024].rearrange("p (v t) -> p v t", v=2)
            OS = [Bf[:, 1024 + i * 1024: 2048 + i * 1024].rearrange("p (v t) -> p v t", v=2) for i in range(2)]
            rden = [Bf[:, 3072:3584], Bf[:, 3584:4096]]
            tmpo = Bf[:, 4096:5120].rearrange("p (v t) -> p v t", v=2)
            hl = {}
            s_hk = s_x[0:2]

            def load_head(h):
                buf = h % 2
                w = [hl.get(("free", h - 2))]
                toks = []
                for m in range(2):
                    toks.append(P.op("sync", lambda e, m=m: e.dma_start(out=KT(buf, m), in_=Kt_d[(2 * h + m) * 128:(2 * h + m + 1) * 128, :]),
                                     waits=w, inc=s_hk[buf], dma=True))
                    toks.append(P.op("sync", lambda e, m=m: e.dma_start(out=QT(buf, m), in_=Qt_d[(2 * h + m) * 128:(2 * h + m + 1) * 128, :]),
                                     waits=w, inc=s_hk[buf], dma=True))
                toks.append(P.op("sync", lambda e: e.dma_start(
                    out=VT(buf), in_=V_d[:, h * 256:(h + 1) * 256].rearrange("(k p) f -> p k f", p=128)),
                    waits=w, inc=s_hk[buf], dma=True))
                hl[("ld", h)] = toks[-1]

            s_sc, s_exp, s_pv, s_aep = s_x[2], s_x[3], s_x[4], s_x[5]
            gsc = {"g": 0, "u": 0}
            exp_tok = {}
            pv_tok = {}
            ep_tok = {}
            os_st = {}
            load_head(0)
            for h in range(16):
                if h + 1 < 16:
                    load_head(h + 1)
                buf = h % 2
                ldtok = hl[("ld", h)]
                for qh in range(2):
                    for m in range(2):
                        u = gsc["u"]
                        gsc["u"] += 1
                        ab = 2 + 3 * (u % 2)
                        accfree = ep_tok.get(u - 2)
                        g0 = gsc["g"]

                        def S_mm(kc):
                            g = g0 + kc
                            return P.op("tensor", lambda e: e.matmul(ps[g % 2][:], KT(buf, m)[:, kc * 128:(kc + 1) * 128],
                                                                     QT(buf, m)[:, qh * 512:(qh + 1) * 512], start=True, stop=True),
                                        waits=[ldtok, exp_tok.get(g - 2)], inc=s_sc)

                        def EXP(kc, sctok):
                            g = g0 + kc
                            exp_tok[g] = P.op("scalar", lambda e: e.activation(out=PTt(g), in_=ps[g % 2][:], func=AF.Exp,
                                                                               scale=128.0 ** -0.5),
                                              waits=[sctok, pv_tok.get(g - 4)], inc=s_exp)

                        def PV(kc):
                            g = g0 + kc
                            w = [exp_tok[g]] + ([accfree] if kc == 0 else [])
                            P.op("tensor", lambda e: e.matmul(ps[ab + 2][:], ones[:], PTt(g), start=(kc == 0), stop=(kc == 31)), waits=w)
                            P.op("tensor", lambda e: e.matmul(ps[ab][:], VT(buf)[:, kc, 0:128], PTt(g), start=(kc == 0), stop=(kc == 31)))
                            pv_tok[g] = P.op("tensor", lambda e: e.matmul(ps[ab + 1][:], VT(buf)[:, kc, 128:256], PTt(g),
                                                                          start=(kc == 0), stop=(kc == 31)), inc=s_pv)

                        EXP(0, S_mm(0))
                        EXP(1, S_mm(1))
                        for kc in range(32):
                            PV(kc)
                            if kc + 2 < 32:
                                EXP(kc + 2, S_mm(kc + 2))
                        gsc["g"] += 32
                        lastpv = pv_tok[g0 + 31]
                        d = P.op("vector", lambda e: e.reciprocal(out=rden[u % 2], in_=ps[ab + 2][:]), waits=[lastpv], inc=s_dve)
                        if m == 0:
                            d = P.op("vector", lambda e: e.tensor_tensor(out=O1[:, 0, :], in0=ps[ab][:], in1=rden[u % 2], op=ALU.mult),
                                     waits=[d, os_st.get("o1")], inc=s_dve)
                            d = P.op("vector", lambda e: e.tensor_tensor(out=O1[:, 1, :], in0=ps[ab + 1][:], in1=rden[u % 2], op=ALU.mult),
                                     waits=[d], inc=s_aep)
                            ep_tok[u] = d
                        else:
                            osb = OS[(u // 2) % 2]
                            d = P.op("vector", lambda e: e.tensor_tensor(out=tmpo[:, 0, :], in0=ps[ab][:], in1=rden[u % 2], op=ALU.mult),
                                     waits=[d], inc=s_dve)
                            d = P.op("vector", lambda e: e.tensor_tensor(out=tmpo[:, 1, :], in0=ps[ab + 1][:], in1=rden[u % 2], op=ALU.mult),
                                     waits=[d], inc=s_aep)
                            ep_tok[u] = d
                            for vc in range(2):
                                d = P.op("vector", lambda e, vc=vc: e.scalar_tensor_tensor(
                                    out=osb[:, vc, :], in0=tmpo[:, vc, :], scalar=scal[:, 0:1], in1=O1[:, vc, :],
                                    op0=ALU.mult, op1=ALU.add), waits=[d, os_st.get((u // 2) % 2)], inc=s_dve)
                            os_st["o1"] = d
                            os_st[(u // 2) % 2] = P.op("sync", lambda e: e.dma_start(
                                out=Op_d[h * 256:(h + 1) * 256, qh * 512:(qh + 1) * 512].rearrange("(v p) t -> p v t", p=128),
                                in_=osb), waits=[d], inc=s_st[(u // 2) % 2], dma=True)
                hl[("free", h)] = pv_tok[gsc["g"] - 1]
            P.barrier()
            if DEBUG["stop"] == "attn":
                return
            C.snap("Op", Op_d[:, 0:128])
            norm_phase(lambda ch, t0, w: Op_d[ch * 128:(ch + 1) * 128, t0:t0 + w], T,
                       lambda c: small[:, c:c + 1], actB, ngroups=16, cpg=2,
                       scale=1.0 / (256 * 0.8 * 0.8), eps_col=1)
            P.barrier()
            gemm(wo_d, list(range(16)), 32, actB, T,
                 resid_epilogue(lambda j, t0, w: xseq[j * 128:(j + 1) * 128, t0:t0 + w], hd_tile))
            P.barrier()

        def rnn_layer():
            norm_phase(lambda ch, t0, w: h_d[ch * 128:(ch + 1) * 128, t0:t0 + w], T, "g_rnn", actA)
            P.barrier()
            halo = Bf[:, 0:96].rearrange("p (c x) -> p c x", c=32)
            us = {"n": 0}

            def u_ep(j, t0, w, bank, mmtok):
                k = us["n"]
                us["n"] += 1
                slot = k % 4
                w_ = [mmtok]
                if st["stv"][slot] is not None:
                    w_.append(st["stv"][slot])
                a1 = P.op("scalar", lambda e: e.activation(out=ht[slot][:, 0:w], in_=bank, func=AF.Identity), waits=w_, inc=s_ev)
                if t0 == 0:
                    P.op("vector", lambda e: e.tensor_copy(out=halo[:, j, 0:1], in_=ht[slot][:, 0:1]), waits=[a1], inc=s_dve)
                else:
                    P.op("vector", lambda e: e.tensor_copy(out=halo[:, j, 1:3], in_=ht[slot][:, w - 2:w]), waits=[a1], inc=s_dve)
                store_tile(ht[slot][:, 0:w], u_d[j * 128:(j + 1) * 128, t0:t0 + w], slot, [a1, (s_dve, s_dve.n)])
                return a1

            def gate_ep(j, t0, w, bank, mmtok):
                k = us["n"]
                us["n"] += 1
                slot = k % 4
                w_ = [mmtok]
                if st["stv"][slot] is not None:
                    w_.append(st["stv"][slot])
                x = ht[slot][:, 0:w]
                t = tmpf[k % 2][:, 0:w]
                a1 = P.op("scalar", lambda e: e.activation(out=x, in_=bank, func=AF.Identity), waits=w_, inc=s_ev)
                d = P.op("vector", lambda e: e.tensor_tensor(out=t, in0=x, in1=x, op=ALU.mult), waits=[a1, us.get(("t", k - 2))], inc=s_dve)
                d = P.op("vector", lambda e: e.tensor_scalar(out=t, in0=t, scalar1=0.044715, scalar2=1.0, op0=ALU.mult, op1=ALU.add),
                         waits=[d], inc=s_dve)
                d = P.op("vector", lambda e: e.tensor_tensor(out=t, in0=t, in1=x, op=ALU.mult), waits=[d], inc=s_dve)
                a2 = P.op("scalar", lambda e: e.activation(out=t, in_=t, func=AF.Sigmoid, scale=1.5957691216057308), waits=[d], inc=s_act)
                d = P.op("vector", lambda e: e.tensor_tensor(out=x, in0=x, in1=t, op=ALU.mult), waits=[a2], inc=s_dve)
                us[("t", k)] = d
                store_tile(x, gg_d[(j - 32) * 128:(j - 31) * 128, t0:t0 + w], slot, [d])
                return a1

            gemm(win_d, list(range(0, 16)), 32, actA, T, u_ep)
            gemm(win_d, list(range(16, 32)), 32, actA, T, gate_ep)
            P.barrier()
            C.snap("u", u_d[:, 0:128])
            C.snap("gg", gg_d[:, 0:128])
            g1 = P.op("gpsimd", lambda e: e.dma_start(out=halo_in, in_=Bf[:, 0:96]), inc=s_g, dma=True)
            c1 = P.op("gpsimd", lambda e: e.collective_compute("AllGather", ALU.bypass, replica_groups=[list(range(8))],
                                                                ins=[halo_in.opt()], outs=[halo_all.opt()]), waits=[g1], inc=s_cc)
            hall = Bf[:, 128:128 + 8 * 96].rearrange("p (r x) -> p r x", r=8)
            g2 = P.op("gpsimd", lambda e: e.dma_start(out=hall, in_=halo_all.rearrange("(r p) x -> p r x", p=128)), waits=[c1],
                      inc=s_g, dma=True)
            hp = Bf[:, 1024:1120].rearrange("p (c x) -> p c x", c=32)
            d = P.op("vector", lambda e: e.memset(Bf[:, 1024:1120], 0.0), waits=[g2], inc=s_dve)
            for r in range(8):
                hr = hall[:, r, :].rearrange("p (c x) -> p c x", c=32)
                d = P.op("vector", lambda e, hr=hr, r=r: e.scalar_tensor_tensor(
                    out=hp[:, :, 0:2], in0=hr[:, :, 1:3], scalar=masks[:, 8 + r:9 + r], in1=hp[:, :, 0:2],
                    op0=ALU.mult, op1=ALU.add), waits=[d], inc=s_dve)
                d = P.op("vector", lambda e, hr=hr, r=r: e.scalar_tensor_tensor(
                    out=hp[:, :, 2:3], in0=hr[:, :, 0:1], scalar=masks[:, 16 + r:17 + r], in1=hp[:, :, 2:3],
                    op0=ALU.mult, op1=ALU.add), waits=[d], inc=s_dve)
            cneg = Bf[:, 1152:1216]
            a = P.op("scalar", lambda e: e.activation(out=cneg, in_=vecs[:, VEC["lam_f"] * 32:(VEC["lam_f"] + 2) * 32], func=AF.Exp, scale=-1.0),
                     waits=[d], inc=s_act)
            a = P.op("scalar", lambda e: e.activation(out=cneg, in_=cneg, func=AF.Ln, bias=epst[:, 3:4]), waits=[a], inc=s_act)
            d = P.op("vector", lambda e: e.tensor_scalar(out=cneg, in0=cneg, scalar1=-8.0, scalar2=None, op0=ALU.mult), waits=[a], inc=s_dve)
            P.barrier()
            Af = ARENA.bitcast(F32)
            up = Af[:, 0:2 * 1027].rearrange("p (c t) -> p c t", c=2)
            uc = Af[:, 2056:2056 + 2048].rearrange("p (c t) -> p c t", c=2)
            ucb = ARENA[:, 8208 + 0: 8208 + 2048].rearrange("p (c t) -> p c t", c=2)
            wgt = ARENA[:, 10256:10256 + 2048].rearrange("p (m i n) -> p m i n", m=4, i=2)
            gt = [Af[:, 6400 + i * 1024: 6400 + (i + 1) * 1024] for i in range(8)]
            car = Bf[:, 2048:2176].rearrange("p (x c) -> p x c", x=4)
            zer = Af[:, 14600:15624]
            d0 = P.op("vector", lambda e: e.memset(zer, 0.0), inc=s_dve)
            s_u, s_w4 = s_x[0], s_x[1]
            prev = {"d": d0, "st": None}

            def rev(ap_t, n=1024):
                return bass.AP(ap_t.tensor, ap_t.offset + (n - 1), [list(ap_t.ap[0]), [-1, n]])

            for nb in range(16):
                lu = P.op("sync", lambda e, nb=nb: e.dma_start(
                    out=up[:, :, 2:1026], in_=u_d[nb * 256:(nb + 1) * 256, :].rearrange("(c p) t -> p c t", p=128)),
                    waits=[prev["d"]], inc=s_u, dma=True)
                lw = P.op("gpsimd", lambda e, nb=nb: e.dma_start(
                    out=ARENA[:, 10256:10256 + 2048], in_=wg_d[nb * 128:(nb + 1) * 128, :]), waits=[prev["d"]], inc=s_w4, dma=True)
                d = P.op("vector", lambda e, nb=nb: e.tensor_copy(out=up[:, :, 0:2], in_=hp[:, 2 * nb:2 * nb + 2, 0:2]),
                         waits=[prev["d"]], inc=s_dve)
                d = P.op("vector", lambda e, nb=nb: e.tensor_copy(out=up[:, :, 1026:1027], in_=hp[:, 2 * nb:2 * nb + 2, 2:3]),
                         waits=[d], inc=s_dve)
                for cc in range(2):
                    ch = 2 * nb + cc
                    d = P.op("vector", lambda e, cc=cc, ch=ch: e.tensor_scalar(
                        out=uc[:, cc, :], in0=up[:, cc, 0:1024], scalar1=vcol("cw0", ch), scalar2=vcol("cb", ch),
                        op0=ALU.mult, op1=ALU.add), waits=[d, lu], inc=s_dve)
                    for tau in range(1, 4):
                        d = P.op("vector", lambda e, cc=cc, ch=ch, tau=tau: e.scalar_tensor_tensor(
                            out=uc[:, cc, :], in0=up[:, cc, tau:tau + 1024], scalar=vcol(f"cw{tau}", ch), in1=uc[:, cc, :],
                            op0=ALU.mult, op1=ALU.add), waits=[d], inc=s_dve)
                    d = P.op("vector", lambda e, cc=cc: e.tensor_copy(out=ucb[:, cc, :], in_=uc[:, cc, :]), waits=[d], inc=s_dve)
                ucd = d
                bias_names = ["ba_f", "bi_f", "ba_b", "bi_b"]
                for oc in range(2):
                    ch = 2 * nb + oc
                    gts = []
                    for mat in range(4):
                        for th in range(2):
                            bk = ps[(mat * 2 + th) % 8][:]
                            for ic in range(2):
                                p = P.op("tensor", lambda e, bk=bk, mat=mat, ic=ic, oc=oc, th=th: e.matmul(
                                    bk, wgt[:, mat, ic, oc * 128:(oc + 1) * 128], ucb[:, ic, th * 512:(th + 1) * 512],
                                    start=(ic == 0), stop=(ic == 1)),
                                    waits=([ucd, lw, (s_act, s_act.n)] if ic == 0 else []), inc=(s_mm if ic == 1 else None))
                            g = gt[mat][:, th * 512:(th + 1) * 512]
                            a = P.op("scalar", lambda e, bk=bk, g=g, mat=mat, ch=ch: e.activation(
                                out=g, in_=bk, func=AF.Sigmoid, bias=vcol(bias_names[mat], ch)),
                                waits=[p, prev["d"]], inc=s_act)
                    for di in range(2):
                        r_t, i_t = gt[2 * di], gt[2 * di + 1]
                        a_t, b_t = gt[4 + 2 * di], gt[5 + 2 * di]
                        cn = cneg[:, di * 32 + ch: di * 32 + ch + 1]
                        a1 = P.op("scalar", lambda e, a_t=a_t, r_t=r_t, cn=cn: e.activation(out=a_t, in_=r_t, func=AF.Exp, scale=cn),
                                  waits=[(s_act, s_act.n), prev["st"]], inc=s_act)
                        d = P.op("vector", lambda e, i_t=i_t, oc=oc: e.tensor_tensor(out=i_t, in0=i_t, in1=uc[:, oc, :], op=ALU.mult),
                                 waits=[(s_act, s_act.n)], inc=s_dve)
                        d = P.op("vector", lambda e, r_t=r_t, a_t=a_t: e.tensor_tensor(out=r_t, in0=a_t, in1=a_t, op=ALU.mult),
                                 waits=[d, a1], inc=s_dve)
                        a2 = P.op("scalar", lambda e, r_t=r_t: e.activation(out=r_t, in_=r_t, func=AF.Sqrt, scale=-1.0, bias=epst[:, 3:4]),
                                  waits=[d], inc=s_act)
                        d = P.op("vector", lambda e, b_t=b_t, r_t=r_t, i_t=i_t: e.tensor_tensor(out=b_t, in0=r_t, in1=i_t, op=ALU.mult),
                                 waits=[a2], inc=s_dve)
                        s1 = P.op("sync", lambda e, a_t=a_t, di=di, ch=ch: e.dma_start(out=ab_d[2 * di][ch * 128:(ch + 1) * 128, :], in_=a_t),
                                  waits=[d], inc=s_st[0], dma=True)
                        s2 = P.op("sync", lambda e, b_t=b_t, di=di, ch=ch: e.dma_start(out=ab_d[2 * di + 1][ch * 128:(ch + 1) * 128, :], in_=b_t),
                                  waits=[d], inc=s_st[1], dma=True)
                        A_in, B_in = (a_t, b_t) if di == 0 else (rev(a_t), rev(b_t))
                        d = P.op("vector", lambda e, r_t=r_t, A_in=A_in, B_in=B_in: e.tensor_tensor_scan(
                            out=r_t, data0=A_in, data1=B_in, initial=0.0, op0=ALU.mult, op1=ALU.add), waits=[d], inc=s_dve)
                        d = P.op("vector", lambda e, di=di, ch=ch, r_t=r_t: e.tensor_copy(out=car[:, 2 * di, ch:ch + 1], in_=r_t[:, 1023:1024]),
                                 waits=[d], inc=s_dve)
                        d = P.op("vector", lambda e, i_t=i_t, A_in=A_in: e.tensor_tensor_scan(
                            out=i_t, data0=A_in, data1=zer, initial=1.0, op0=ALU.mult, op1=ALU.add), waits=[d], inc=s_dve)
                        d = P.op("vector", lambda e, di=di, ch=ch, i_t=i_t: e.tensor_copy(out=car[:, 2 * di + 1, ch:ch + 1], in_=i_t[:, 1023:1024]),
                                 waits=[d, s1, s2], inc=s_dve)
                        prev["d"] = d
                        prev["st"] = s2
            P.barrier()
            if DEBUG["stop"] == "rnnA":
                return
            for i_ in range(4):
                C.snap(f"ab{i_}", ab_d[i_][:, 0:128])
            g1 = P.op("gpsimd", lambda e: e.dma_start(out=car_in, in_=Bf[:, 2048:2176]), inc=s_g, dma=True)
            c1 = P.op("gpsimd", lambda e: e.collective_compute("AllGather", ALU.bypass, replica_groups=[list(range(8))],
                                                                ins=[car_in.opt()], outs=[car_all.opt()]), waits=[g1], inc=s_cc)
            call = Bf[:, 2304:2304 + 1024].rearrange("p (r x c) -> p r x c", r=8, x=4)
            g2 = P.op("gpsimd", lambda e: e.dma_start(out=Bf[:, 2304:2304 + 1024].rearrange("p (r f) -> p r f", r=8),
                                                      in_=car_all.rearrange("(r p) f -> p r f", p=128)), waits=[c1], inc=s_g, dma=True)
            E = carr[:, 0:32]
            Cf = carr[:, 32:64]
            Cb = carr[:, 64:96]
            d = P.op("vector", lambda e: e.memset(carr[:], 0.0), waits=[g2], inc=s_dve)
            for r in range(8):
                if r == 4:
                    d = P.op("vector", lambda e: e.memset(E, 0.0), waits=[d], inc=s_dve)
                d = P.op("vector", lambda e, r=r: e.scalar_tensor_tensor(out=Cf, in0=E, scalar=masks[:, r:r + 1], in1=Cf,
                                                                          op0=ALU.mult, op1=ALU.add), waits=[d], inc=s_dve)
                d = P.op("vector", lambda e, r=r: e.tensor_tensor(out=E, in0=E, in1=call[:, r, 1, :], op=ALU.mult), waits=[d], inc=s_dve)
                d = P.op("vector", lambda e, r=r: e.tensor_tensor(out=E, in0=E, in1=call[:, r, 0, :], op=ALU.add), waits=[d], inc=s_dve)
            d = P.op("vector", lambda e: e.memset(E, 0.0), waits=[d], inc=s_dve)
            for r in range(7, -1, -1):
                if r == 3:
                    d = P.op("vector", lambda e: e.memset(E, 0.0), waits=[d], inc=s_dve)
                d = P.op("vector", lambda e, r=r: e.scalar_tensor_tensor(out=Cb, in0=E, scalar=masks[:, r:r + 1], in1=Cb,
                                                                          op0=ALU.mult, op1=ALU.add), waits=[d], inc=s_dve)
                d = P.op("vector", lambda e, r=r: e.tensor_tensor(out=E, in0=E, in1=call[:, r, 3, :], op=ALU.mult), waits=[d], inc=s_dve)
                d = P.op("vector", lambda e, r=r: e.tensor_tensor(out=E, in0=E, in1=call[:, r, 2, :], op=ALU.add), waits=[d], inc=s_dve)
            P.barrier()
            wt_ = [Af[:, i * 1024:(i + 1) * 1024] for i in range(12)]
            pb = {"d": None}
            for ch in range(32):
                o = (ch % 2) * 6
                ta, tb_, tc, td, tg, tf_ = wt_[o], wt_[o + 1], wt_[o + 2], wt_[o + 3], wt_[o + 4], wt_[o + 5]
                w0 = [pb.get(ch - 2)]
                l = []
                sset = (s_ld if ch % 2 == 0 else s_st)
                for i_, tl in enumerate([ta, tb_, tc, td]):
                    l.append(P.op("sync", lambda e, i_=i_, tl=tl, ch=ch: e.dma_start(out=tl, in_=ab_d[i_][ch * 128:(ch + 1) * 128, :]),
                                  waits=w0, inc=sset[i_], dma=True))
                lg = P.op("sync", lambda e, tg=tg, ch=ch: e.dma_start(out=tg, in_=gg_d[ch * 128:(ch + 1) * 128, :]), waits=w0,
                          inc=s_x[ch % 2], dma=True)
                d = P.op("vector", lambda e, ta=ta, tb_=tb_, ch=ch: e.tensor_tensor_scan(
                    out=tf_, data0=ta, data1=tb_, initial=Cf[:, ch:ch + 1], op0=ALU.mult, op1=ALU.add), waits=[l[0], l[1]], inc=s_dve)
                d = P.op("vector", lambda e, tc=tc, td=td, ch=ch: e.tensor_tensor_scan(
                    out=rev(ta), data0=rev(tc), data1=rev(td), initial=Cb[:, ch:ch + 1], op0=ALU.mult, op1=ALU.add),
                    waits=[l[2], l[3], d], inc=s_dve)
                d = P.op("vector", lambda e, ta=ta, tb_=tb_: e.tensor_tensor(out=ta, in0=ta, in1=tf_, op=ALU.add), waits=[d], inc=s_dve)
                d = P.op("vector", lambda e, ta=ta, tg=tg, ch=ch: e.tensor_tensor(out=B3[:, ch, :], in0=ta, in1=tg, op=ALU.mult),
                         waits=[d, lg], inc=s_dve)
                pb[ch] = d
            P.barrier()
            gemm(wout_d, list(range(16)), 32, actB, T, resid_epilogue(hd_tile, hd_tile))
            P.barrier()

        snaps = {"n": 0}

        def snap(name, src_ap, rows=D, dt=F32):
            if not DEBUG["snap"]:
                return
            t = nc.dram_tensor("snap_" + name, [rows, 128], dt, kind="ExternalOutput").ap()
            P.op("sync", lambda e: e.dma_start(out=t, in_=src_ap), inc=s_ld[snaps["n"] % 4], dma=True)
            snaps["n"] += 1
            P.barrier()
        C.snap = snap
        attn_layer()
        snap("h_attn", h_d[:, 0:128])
        if DEBUG["stop"] is None or DEBUG["stop"] in ("l0", "all", "rnnA", "l1"):
            xattn_block(0)
            snap("h_x0", h_d[:, 0:128])
            mlp_block(0)
            snap("h_m0", h_d[:, 0:128])
        if DEBUG["stop"] in (None, "all", "rnnA", "l1"):
            rnn_layer()
            snap("h_rnn", h_d[:, 0:128])
        if DEBUG["stop"] in (None, "all"):
            xattn_block(1)
            snap("h_x1", h_d[:, 0:128])
            mlp_block(1)
            snap("h_m1", h_d[:, 0:128])
        if DEBUG["stop"] in (None, "all"):
            norm_phase(lambda ch, t0, w: h_d[ch * 128:(ch + 1) * 128, t0:t0 + w], T, "g_fin", None,
                       store_fn=lambda ch, t0, w: out_d[ch * 128:(ch + 1) * 128, t0:t0 + w])
        else:
            P.op("sync", lambda e: e.dma_start(out=out_d[0:128, 0:512], in_=ht[0][:]), inc=s_st[0], dma=True)
        P.barrier()
        P.emit()
    return nc


def tile_w(W):
    Kd, N = W.shape
    KC, NJ = Kd // 128, N // NG
    return np.ascontiguousarray(W.reshape(KC, 128, NJ, NG).transpose(2, 1, 0, 3)).reshape(NJ * 128, KC * NG)


def vec32(v):
    return np.ascontiguousarray(np.asarray(v, np.float32).reshape(32, 128).T)


def prep_inputs(inp):
    f = lambda a: np.asarray(a, dtype=np.float32)
    x = f(inp["x"]); mem = f(inp["mem"]); pos = np.asarray(inp["positions"]).astype(np.int32)
    shared = {}
    shared["wqkv"] = tile_w(f(inp["attn_w_qkv"])[0])
    shared["wo"] = tile_w(f(inp["attn_w_o"])[0])
    shared["win"] = tile_w(f(inp["rnn_w_in"])[0])
    shared["wout"] = tile_w(f(inp["rnn_w_out"])[0])
    for i in range(2):
        shared[f"xq{i}"] = tile_w(f(inp["xattn_w_q"])[i])
        shared[f"xkv{i}"] = tile_w(f(inp["xattn_w_kv"])[i])
        shared[f"xo{i}"] = tile_w(f(inp["xattn_w_o"])[i])
        shared[f"w1_{i}"] = tile_w(f(inp["mlp_w1"])[i])
        w2 = f(inp["mlp_w2"])[i]
        shared[f"w2_{i}"] = np.concatenate([tile_w(w2[g * 4096:(g + 1) * 4096]) for g in range(4)], axis=0)
    mats = [f(inp[n])[0] for n in ["rnn_wa_f", "rnn_wi_f", "rnn_wa_b", "rnn_wi_b"]]
    wg = np.stack(mats, axis=1)
    wg = wg.reshape(16, 4, 2, 128, 256).transpose(0, 3, 1, 2, 4)
    shared["wg"] = np.ascontiguousarray(wg).reshape(16 * 128, 4 * 2 * 256)
    vn = [inp["attn_norm_g"][0], inp["rnn_norm_g"][0], inp["xattn_norm_g"][0], inp["xattn_norm_g"][1],
          inp["xattn_mem_g"][0], inp["xattn_mem_g"][1], inp["mlp_norm_g"][0], inp["mlp_norm_g"][1], inp["final_g"],
          inp["rnn_conv_w"][0][0], inp["rnn_conv_w"][0][1], inp["rnn_conv_w"][0][2], inp["rnn_conv_w"][0][3],
          inp["rnn_conv_b"][0], inp["rnn_lam_f"][0], inp["rnn_lam_b"][0],
          np.asarray(inp["rnn_ba_f"][0]).reshape(-1), np.asarray(inp["rnn_bi_f"][0]).reshape(-1),
          np.asarray(inp["rnn_ba_b"][0]).reshape(-1), np.asarray(inp["rnn_bi_b"][0]).reshape(-1)]
    shared["vecs"] = np.ascontiguousarray(np.concatenate([vec32(v) for v in vn], axis=1))
    small = np.zeros((128, 8 + 512 + 32), np.float32)
    small[:, 0:2] = f(inp["attn_subln_g"])[0].reshape(2, 128).T
    invf = (500000.0 ** (-np.arange(0, 32, 2, dtype=np.float32) / 32)).astype(np.float32)
    small[0:32, 2] = np.concatenate([invf, invf])
    small[0:16, 3] = -1.0
    small[16:32, 3] = 1.0
    lamv = np.concatenate([f(inp["attn_lambda_q1"])[0], f(inp["attn_lambda_k1"])[0],
                           f(inp["attn_lambda_q2"])[0], f(inp["attn_lambda_k2"])[0]])
    small[:, 8:520] = lamv[None, :]
    Pm = np.zeros((32, 32), np.float32)
    for m in range(32):
        Pm[(m + 16) % 32, m] = 1.0
    small[0:32, 520:552] = Pm
    shared["small"] = small
    flat = {n: shared.pop(n).reshape(-1, 8192) for n, _ in WNAMES}
    for n, rows in WNAMES:
        assert flat[n].shape[0] == rows, (n, flat[n].shape)
    shards = [[] for _ in range(8)]
    for n, rows in WNAMES:
        for p in range(0, rows, 2048):
            pr = min(2048, rows - p)
            m = pr // 8
            for c in range(8):
                shards[c].append(flat[n][p + c * m: p + (c + 1) * m])
    shards = [np.concatenate(sh, axis=0) for sh in shards]
    in_maps = []
    for c in range(8):
        b, q = c // 4, c % 4
        order = [q] + [k for k in range(4) if k != q]
        idx = np.concatenate([np.arange(k * T, (k + 1) * T) for k in order])
        m = dict(shared)
        m["wsh"] = shards[c]
        m["xseq"] = np.ascontiguousarray(x[b][idx].T)
        m["pos"] = np.ascontiguousarray(pos[b][idx][None, :])
        m["memT"] = np.ascontiguousarray(mem[b].T)
        mk = np.zeros((128, 24), np.float32)
        mk[:, c] = 1.0
        if q > 0:
            mk[:, 8 + c - 1] = 1.0
        if q < 3:
            mk[:, 16 + c + 1] = 1.0
        m["masks"] = mk
        in_maps.append(m)
    return in_maps


_NC = {}


def kernel(**inputs):
    in_maps = prep_inputs(inputs)
    key = (tuple(DEBUG["dump"]), DEBUG["stop"])
    nc = build()
    res = run_bass_kernel_spmd(nc, in_maps, core_ids=list(range(8)))
    out = np.zeros((2, SEQ, D), np.float32)
    for c in range(8):
        b, q = c // 4, c % 4
        out[b, q * T:(q + 1) * T, :] = res.results[c]["out"].T
    DEBUG["last"] = res
    return out
```

```python
import contextlib
import math
import numpy as np
import concourse.bass as bass
import concourse.mybir as mybir
from concourse.bass_utils import run_bass_kernel_spmd

F32 = mybir.dt.float32
BF16 = mybir.dt.bfloat16
I32 = mybir.dt.int32
ALU = mybir.AluOpType
AF = mybir.ActivationFunctionType
AX = mybir.AxisListType

D = 4096
T = 1024
SEQ = 4096
EPS = 1e-6
NG = 256
DEBUG = {"dump": [], "stop": None, "snap": False}
WNAMES = [("wqkv", 6144), ("wo", 2048), ("xkv0", 512), ("xq0", 256), ("xo0", 256),
          ("w1_0", 8192), ("w2_0", 8192), ("win", 4096), ("wg", 512), ("wout", 2048),
          ("xkv1", 512), ("xq1", 256), ("xo1", 256), ("w1_1", 8192), ("w2_1", 8192)]
WOFF = {}
PIECES = []
_r = 0
for _n, _rows in WNAMES:
    WOFF[_n] = (_r, _rows)
    for _p in range(0, _rows, 2048):
        PIECES.append((_n, _p, min(2048, _rows - _p)))
    _r += _rows
WROWS = _r


class S:
    def __init__(self, h, name):
        self.h = h
        self.n = 0
        self.name = name


class Rec:
    def __init__(self):
        self.calls = []

    def __getattr__(self, name):
        def f(*a, **k):
            self.calls.append((name, a, k))
            return None
        return f


class Prog:
    ENG = ["sync", "gpsimd", "tensor", "scalar", "vector"]

    def __init__(self, nc):
        self.nc = nc
        self.q = {e: [] for e in self.ENG}
        self.stack = contextlib.ExitStack()
        self.sems = []
        self.nbar = 0
        self.s_bar = None

    def sem(self, name, nobar=False):
        h = self.stack.enter_context(self.nc.semaphore(name))
        s = S(h, name)
        if not nobar:
            self.sems.append(s)
        return s

    def sbuf(self, name, shape, dt):
        return self.stack.enter_context(self.nc.sbuf_tensor(name, shape, dt))

    def psum(self, name, shape, dt):
        return self.stack.enter_context(self.nc.psum_tensor(name, shape, dt))

    def op(self, eng, fn, waits=(), inc=None, dma=False):
        rec = Rec()
        fn(rec)
        calls = rec.calls
        assert len(calls) == 1
        fn = calls[0]
        val = None
        if inc is not None:
            inc.n += 16 if dma else 1
            val = inc.n
            assert val < 65000, inc.name
        self.q[eng].append((fn, tuple(w for w in waits if w is not None), inc, 16 if dma else 1))
        return (inc, val) if inc is not None else None

    def barrier(self):
        cur = [(s, s.n) for s in self.sems if s.n > 0 and s is not self.s_bar]
        self.nbar += 1
        for e in self.ENG:
            self.op(e, lambda eng: eng.sem_inc(self.s_bar.h, 1), waits=cur)
        self.s_bar.n += len(self.ENG)
        tgt = self.s_bar.n
        for e in self.ENG:
            self.q[e].append((None, ((self.s_bar, tgt),), None, 0))

    def emit(self):
        with self.nc.Block() as block:
            for e in self.ENG:
                items = self.q[e]

                def body(eng, items=items):
                    last = {}
                    for fn, waits, inc, n in items:
                        for (s, v) in waits:
                            if v is None or v <= 0:
                                continue
                            if last.get(s.name, 0) >= v:
                                continue
                            eng.wait_ge(s.h, v)
                            last[s.name] = v
                        if fn is None:
                            continue
                        name, a_, k_ = fn
                        ins = getattr(eng, name)(*a_, **k_)
                        if inc is not None:
                            ins.then_inc(inc.h, n)

                getattr(block, e)(body)


def lambda_init(layer_idx):
    return 0.8 - 0.6 * float(np.exp(-0.3 * layer_idx))


class K:
    def __init__(self, nc):
        self.nc = nc
        self.P = Prog(nc)
        self.din = {}
        self.dout = {}
        self.scr = {}

    def inp(self, name, shape, dt=F32):
        t = self.nc.dram_tensor(name, list(shape), dt, kind="ExternalInput")
        self.din[name] = t
        return t.ap()

    def scratch(self, name, shape, dt):
        kind = "ExternalOutput" if name in DEBUG["dump"] else "Internal"
        t = self.nc.dram_tensor(name, list(shape), dt, kind=kind)
        self.scr[name] = t
        return t.ap()


def build():
    nc = bass.Bass("TRN2", target_bir_lowering=False)
    C = K(nc)
    P = C.P
    xseq = C.inp("xseq", [D, SEQ])
    pos_d = C.inp("pos", [1, SEQ], I32)
    memT = C.inp("memT", [D, 256])
    masks_d = C.inp("masks", [128, 24])
    vecs_d = C.inp("vecs", [128, 20 * 32])
    small_d = C.inp("small", [128, 8 + 512 + 32])
    wsh = C.inp("wsh", [WROWS // 8, 8192])
    wsh_i = nc.dram_tensor("wsh_i", [WROWS // 8, 8192], F32).ap()
    wall = {n_: nc.dram_tensor("wall_" + n_, [rows_, 8192], F32).ap() for n_, rows_ in WNAMES}

    def wv(name, cols=8192):
        v = wall[name]
        if cols != 8192:
            v = v.rearrange("r (a c) -> (r a) c", c=cols)
        return v
    wqkv_d, wo_d, win_d, wout_d = wv("wqkv"), wv("wo"), wv("win"), wv("wout")
    xq_d = [wv(f"xq{i}") for i in range(2)]
    xkv_d = [wv(f"xkv{i}") for i in range(2)]
    xo_d = [wv(f"xo{i}", 4 * NG) for i in range(2)]
    w1_d = [wv(f"w1_{i}") for i in range(2)]
    w2_d = [wv(f"w2_{i}") for i in range(2)]
    wg_d = wv("wg", 4 * 2 * 256)
    out_d = nc.dram_tensor("out", [D, T], F32, kind="ExternalOutput").ap()

    h_d = C.scratch("h_d", [D, T], F32)
    Kt_d = C.scratch("Kt_d", [D, SEQ], BF16)
    V_d = C.scratch("V_d", [SEQ, D], BF16)
    Qt_d = C.scratch("Qt_d", [D, T], BF16)
    Op_d = C.scratch("Op_d", [D, T], F32)
    u_d = C.scratch("u_d", [D, T], F32)
    gg_d = C.scratch("gg_d", [D, T], F32)
    ab_d = [C.scratch(f"ab{i}_d", [D, T], F32) for i in range(4)]
    halo_in = C.scratch("halo_in", [128, 96], F32)
    halo_all = C.scratch("halo_all", [8 * 128, 96], F32)
    car_in = C.scratch("car_in", [128, 128], F32)
    car_all = C.scratch("car_all", [8 * 128, 128], F32)

    with P.stack:
        P.s_bar = P.sem("s_bar")
        ARENA = P.sbuf("arena", [128, 57344], BF16)
        Bt = P.sbuf("Bt", [128, 32768], BF16)
        A3 = ARENA[:, 0:32768].rearrange("p (k t) -> p k t", k=32)
        B3 = Bt[:].rearrange("p (k t) -> p k t", k=32)
        Bf = Bt.bitcast(F32)
        WS = [ARENA[:, 32768 + i * 8192: 32768 + (i + 1) * 8192] for i in range(3)]
        ht = [P.sbuf(f"ht{i}", [128, 512], F32) for i in range(4)]
        sq = [P.sbuf(f"sq{i}", [128, 512], BF16) for i in range(2)]
        rstd = P.sbuf("rstd", [128, 1024], F32)
        tmpf = [P.sbuf(f"tmpf{i}", [128, 512], F32) for i in range(2)]
        ob = [P.sbuf(f"ob{i}", [128, 512], BF16) for i in range(2)]
        vecs = P.sbuf("vecs_sb", [128, 20 * 32], F32)
        small = P.sbuf("small_sb", [128, 8 + 512 + 32], F32)
        masks = P.sbuf("masks_sb", [128, 24], F32)
        ones = P.sbuf("ones", [128, 128], BF16)
        epst = P.sbuf("epst", [128, 4], F32)
        scal = P.sbuf("scal", [128, 16], F32)
        carr = P.sbuf("carr", [128, 96], F32)
        ps = [P.psum(f"ps{i}", [128, 512], F32) for i in range(8)]

        s_ld = [P.sem(f"s_ld{i}") for i in range(4)]
        s_st = [P.sem(f"s_st{i}") for i in range(4)]
        s_wld = [P.sem(f"s_wld{i}") for i in range(3)]
        s_mm = P.sem("s_mm")
        s_ev = P.sem("s_ev")
        s_act = P.sem("s_act")
        s_dve = P.sem("s_dve")
        s_pe2 = P.sem("s_pe2")
        s_misc = P.sem("s_misc")
        s_g = P.sem("s_g")
        s_cc = P.sem("s_cc")
        s_x = [P.sem(f"s_x{i}") for i in range(6)]

        st = {"wuse": 0, "wfree": {}, "ld": [0] * 4, "stv": [None] * 4, "wld": [0] * 3,
              "bank": 0, "bankfree": {}, "grp": 0}

        VEC = {n: i for i, n in enumerate(
            ["g_attn", "g_rnn", "g_x0", "g_x1", "g_m0", "g_m1", "g_mlp0", "g_mlp1", "g_fin",
             "cw0", "cw1", "cw2", "cw3", "cb", "lam_f", "lam_b", "ba_f", "bi_f", "ba_b", "bi_b"])}

        def vcol(name, kc):
            i = VEC[name]
            return vecs[:, i * 32 + kc: i * 32 + kc + 1]

        P.op("sync", lambda e: e.dma_start(out=vecs[:], in_=vecs_d), inc=s_misc, dma=True)
        P.op("sync", lambda e: e.dma_start(out=small[:], in_=small_d), inc=s_misc, dma=True)
        P.op("sync", lambda e: e.dma_start(out=masks[:], in_=masks_d), inc=s_misc, dma=True)
        so = 0
        PTOK = {}
        for pi, (wn, r0, n) in enumerate(PIECES):
            m = n // 8
            scp = P.sem(f"s_cp{pi}", nobar=True)
            sag = P.sem(f"s_ag{pi}", nobar=True)
            ct = P.op("sync", lambda e, so=so, m=m: e.dma_start(out=wsh_i[so:so + m, :], in_=wsh[so:so + m, :]), inc=scp, dma=True)
            PTOK[(wn, r0 // 2048)] = P.op("gpsimd", lambda e, so=so, m=m, r0=r0, n=n, wn=wn: e.collective_compute(
                "AllGather", ALU.bypass, replica_groups=[list(range(8))],
                ins=[wsh_i[so:so + m, :].opt()], outs=[wall[wn][r0:r0 + n, :].opt()]),
                waits=[ct], inc=sag)
            so += m
        P.op("vector", lambda e: e.memset(ones[:], 1.0), inc=s_dve)
        P.op("vector", lambda e: e.memset(epst[:, 0:1], EPS), inc=s_dve)
        P.op("vector", lambda e: e.memset(epst[:, 1:2], EPS / (0.8 * 0.8)), inc=s_dve)
        P.op("vector", lambda e: e.memset(epst[:, 2:3], -math.pi), inc=s_dve)
        P.op("vector", lambda e: e.memset(epst[:, 3:4], 1.0), inc=s_dve)
        P.barrier()

        def load_tile(slot, src_ap, width=512, extra_waits=()):
            w = list(extra_waits)
            if st["stv"][slot] is not None:
                w.append(st["stv"][slot])
            return P.op("sync", lambda e: e.dma_start(out=ht[slot][:, 0:width], in_=src_ap), waits=w,
                        inc=s_ld[slot], dma=True)

        def store_tile(src_sb_ap, dst_ap, slot, waits):
            tok = P.op("sync", lambda e: e.dma_start(out=dst_ap, in_=src_sb_ap), waits=waits,
                       inc=s_st[slot], dma=True)
            st["stv"][slot] = tok
            return tok

        def norm_phase(src_fn, ntok, gname, dst_fn, ngroups=1, cpg=32, scale=1.0 / D, eps_col=0,
                       store_fn=None):
            TW = min(512, ntok)
            nth = ntok // TW
            for g in range(ngroups):
                gguard = (s_dve, s_dve.n)
                gguard2 = (s_pe2, s_pe2.n)
                use_free = {}
                sq_free = {}
                cnt = 0
                ss_done = []
                for th in range(nth):
                    bank = ps[4 + th]
                    last_pe = None
                    for c in range(cpg):
                        ch = g * cpg + c
                        slot = cnt % 4
                        lt = load_tile(slot, src_fn(ch, th * TW, TW), TW,
                                       extra_waits=[use_free.get(cnt - 4)] + ([gguard] if cnt < 4 else []))
                        at = P.op("scalar", lambda e, slot=slot, k=cnt: e.activation(
                            out=sq[k % 2][:, 0:TW], in_=ht[slot][:, 0:TW], func=AF.Square),
                            waits=[lt, sq_free.get(cnt - 2)] + ([gguard2] if cnt < 2 else []), inc=s_act)
                        use_free[cnt] = at
                        pt = P.op("tensor", lambda e, bank=bank, k=cnt, c=c: e.matmul(
                            bank[:, 0:TW], ones[:], sq[k % 2][:, 0:TW], start=(c == 0), stop=(c == cpg - 1)),
                            waits=[at], inc=s_pe2)
                        sq_free[cnt] = pt
                        last_pe = pt
                        cnt += 1
                    a1 = P.op("scalar", lambda e, bank=bank, th=th: e.activation(
                        out=tmpf[th % 2][:, 0:TW], in_=bank[:, 0:TW], func=AF.Sqrt,
                        bias=epst[:, eps_col:eps_col + 1], scale=scale), waits=[last_pe, gguard], inc=s_act)
                    d1 = P.op("vector", lambda e, th=th: e.reciprocal(
                        out=rstd[:, th * TW:(th + 1) * TW], in_=tmpf[th % 2][:, 0:TW]), waits=[a1, gguard], inc=s_dve)
                    ss_done.append(d1)
                cnt2 = 0
                dfree = {}
                for th in range(nth):
                    for c in range(cpg):
                        ch = g * cpg + c
                        slot = cnt2 % 4
                        ew = [dfree.get(cnt2 - 4)]
                        if cnt2 < 4:
                            ew.append((s_act, s_act.n))
                        lt = load_tile(slot, src_fn(ch, th * TW, TW), TW, extra_waits=ew)
                        gcol = vcol(gname, c) if isinstance(gname, str) else gname(c)
                        if store_fn is None:
                            dt_ = P.op("vector", lambda e, slot=slot, ch=ch, th=th, gcol=gcol: e.scalar_tensor_tensor(
                                out=dst_fn(ch, th * TW, TW), in0=ht[slot][:, 0:TW], scalar=gcol,
                                in1=rstd[:, th * TW:(th + 1) * TW], op0=ALU.mult, op1=ALU.mult),
                                waits=[lt, ss_done[th]], inc=s_dve)
                            dfree[cnt2] = dt_
                        else:
                            dt_ = P.op("vector", lambda e, slot=slot, th=th, gcol=gcol: e.scalar_tensor_tensor(
                                out=ht[slot][:, 0:TW], in0=ht[slot][:, 0:TW], scalar=gcol,
                                in1=rstd[:, th * TW:(th + 1) * TW], op0=ALU.mult, op1=ALU.mult),
                                waits=[lt, ss_done[th]], inc=s_dve)
                            store_tile(ht[slot][:, 0:TW], store_fn(ch, th * TW, TW), slot, [dt_])
                        cnt2 += 1
                P.op("vector", lambda e: e.memset(scal[:, 15:16], 0.0), waits=[(s_dve, s_dve.n)], inc=s_dve)

        def wload(wd, jg, ncols, wname, pbase):
            k = st["wuse"]
            slot = k % 3
            w = [st["wfree"].get(k - 3), PTOK[(wname, pbase + (jg * 128 * ncols) // (2048 * 8192))]]
            tok = P.op("gpsimd", lambda e: e.dma_start(out=WS[slot][:, 0:ncols], in_=wd[jg * 128:(jg + 1) * 128, 0:ncols]),
                       waits=w, inc=s_wld[slot], dma=True)
            st["wuse"] += 1
            return k, slot, tok

        def gemm(wd, groups, KC, act_fn, ntok, epilogue, moving=False, prefetch=2, wname=None, pbase=0):
            ncols = KC * NG
            pend = []
            gi = 0
            loads = {}
            for idx in range(min(prefetch, len(groups))):
                loads[idx] = wload(wd, groups[idx], ncols, wname, pbase)
            for idx, jg in enumerate(groups):
                if idx + prefetch < len(groups):
                    loads[idx + prefetch] = wload(wd, groups[idx + prefetch], ncols, wname, pbase)
                k, slot, wtok = loads.pop(idx)
                W3 = WS[slot][:, 0:ncols].rearrange("p (k n) -> p k n", k=KC)
                last = None
                if not moving:
                    TW = min(512, ntok)
                    for jl in range(NG // 128):
                        for th in range(ntok // TW):
                            b = st["bank"] % 4
                            bfree = st["bankfree"].get(st["bank"] - 4)
                            bank = ps[b][:, 0:TW]
                            for kc in range(KC):
                                last = P.op("tensor", lambda e, bank=bank, W3=W3, kc=kc, jl=jl, th=th: e.matmul(
                                    bank, W3[:, kc, jl * 128:(jl + 1) * 128], act_fn(kc, th * TW, TW),
                                    start=(kc == 0), stop=(kc == KC - 1)),
                                    waits=([wtok, bfree] if kc == 0 else []),
                                    inc=(s_mm if kc == KC - 1 else None))
                            st["bankfree"][st["bank"]] = epilogue(jg * (NG // 128) + jl, th * TW, TW, bank, last)
                            st["bank"] += 1
                else:
                    for tb in range(ntok // 128):
                        b = st["bank"] % 4
                        bfree = st["bankfree"].get(st["bank"] - 4)
                        bank = ps[b][:, 0:NG]
                        for kc in range(KC):
                            last = P.op("tensor", lambda e, bank=bank, W3=W3, kc=kc, tb=tb: e.matmul(
                                bank, act_fn(kc, tb * 128, 128), W3[:, kc, :],
                                start=(kc == 0), stop=(kc == KC - 1)),
                                waits=([wtok, bfree] if kc == 0 else []),
                                inc=(s_mm if kc == KC - 1 else None))
                        st["bankfree"][st["bank"]] = epilogue(jg, tb * 128, 128, bank, last)
                        st["bank"] += 1
                st["wfree"][k] = last

        rs = {"n": 0}

        def resid_epilogue(src_fn, dst_fn):
            def ep(j, t0, w, bank, mmtok):
                slot = rs["n"] % 4
                rs["n"] += 1
                lt = load_tile(slot, src_fn(j, t0, w), w)
                dt_ = P.op("vector", lambda e: e.tensor_tensor(out=ht[slot][:, 0:w], in0=bank, in1=ht[slot][:, 0:w], op=ALU.add),
                           waits=[mmtok, lt], inc=s_ev)
                store_tile(ht[slot][:, 0:w], dst_fn(j, t0, w), slot, [dt_])
                return dt_
            return ep

        def hd_tile(j, t0, w):
            return h_d[j * 128:(j + 1) * 128, t0:t0 + w]

        actA = lambda kc, t0, w: A3[:, kc, t0:t0 + w]
        actB = lambda kc, t0, w: B3[:, kc, t0:t0 + w]

        def copy_epilogue(dst_fn, scale=1.0, eng="scalar"):
            def ep(j, t0, w, bank, mmtok):
                if eng == "scalar":
                    return P.op("scalar", lambda e: e.activation(out=dst_fn(j, t0, w), in_=bank, func=AF.Identity, scale=scale),
                                waits=[mmtok], inc=s_ev)
                return P.op("vector", lambda e: e.tensor_copy(out=dst_fn(j, t0, w), in_=bank), waits=[mmtok], inc=s_ev)
            return ep

        def xattn_block(i):
            gx = "g_x0" if i == 0 else "g_x1"
            gm = "g_m0" if i == 0 else "g_m1"
            Mn = lambda kc, t0, w: B3[:, kc, t0:t0 + w]
            norm_phase(lambda ch, t0, w: memT[ch * 128:(ch + 1) * 128, t0:t0 + w], 256, gm, Mn)
            P.barrier()
            Kx = lambda hx, t0, w: B3[:, hx, 256 + t0:256 + t0 + w]
            gemm(xkv_d[i], [0, 1], 32, Mn, 256, copy_epilogue(lambda j, t0, w: Kx(j, t0, w)), wname=f"xkv{i}")
            Vx = lambda tb, f0, w: B3[:, 4 + tb, 256 + f0:256 + f0 + w]
            gemm(xkv_d[i], [2, 3], 32, Mn, 256,
                 copy_epilogue(lambda jg, t0, w: Vx(t0 // 128, (jg - 2) * NG, NG), eng="vector"), moving=True, wname=f"xkv{i}")
            P.barrier()
            norm_phase(lambda ch, t0, w: h_d[ch * 128:(ch + 1) * 128, t0:t0 + w], T, gx, actA)
            P.barrier()
            Qx = lambda hx, t0, w: B3[:, 8 + hx, t0:t0 + w]
            gemm(xq_d[i], [0, 1], 32, actA, T, copy_epilogue(lambda j, t0, w: Qx(j, t0, w)), wname=f"xq{i}")
            P.barrier()
            Ox = lambda hx, t0, w: B3[:, 12 + hx, t0:t0 + w]
            PT = lambda kc: B3[:, 16 + kc, 0:512]
            for hx in range(4):
                for qh in range(2):
                    toks = []
                    for kc in range(2):
                        m1 = P.op("tensor", lambda e, kc=kc: e.matmul(ps[kc][:], Kx(hx, kc * 128, 128), Qx(hx, qh * 512, 512),
                                                                       start=True, stop=True),
                                  waits=[(s_dve, s_dve.n), (s_act, s_act.n)], inc=s_mm)
                        a1 = P.op("scalar", lambda e, kc=kc: e.activation(out=PT(kc), in_=ps[kc][:], func=AF.Exp,
                                                                         scale=128.0 ** -0.5),
                                  waits=[m1, (s_pe2, s_pe2.n)], inc=s_act)
                        toks.append(a1)
                    for kc in range(2):
                        P.op("tensor", lambda e, kc=kc: e.matmul(ps[2][:], ones[:], PT(kc), start=(kc == 0), stop=(kc == 1)),
                             waits=[toks[kc]], inc=s_pe2)
                        p2 = P.op("tensor", lambda e, kc=kc: e.matmul(ps[3][:], Vx(kc, hx * 128, 128), PT(kc),
                                                                       start=(kc == 0), stop=(kc == 1)), inc=s_pe2)
                    d1 = P.op("vector", lambda e: e.reciprocal(out=tmpf[0][:], in_=ps[2][:]), waits=[p2], inc=s_dve)
                    P.op("vector", lambda e: e.tensor_tensor(out=Ox(hx, qh * 512, 512), in0=ps[3][:], in1=tmpf[0][:], op=ALU.mult),
                         waits=[d1], inc=s_dve)
            P.barrier()
            gemm(xo_d[i], list(range(16)), 4, lambda kc, t0, w: Ox(kc, t0, w), T, resid_epilogue(hd_tile, hd_tile), wname=f"xo{i}")
            P.barrier()

        def mlp_block(i):
            g = "g_mlp0" if i == 0 else "g_mlp1"
            norm_phase(lambda ch, t0, w: h_d[ch * 128:(ch + 1) * 128, t0:t0 + w], T, g, actA)
            P.barrier()
            rr = {"n": 0}

            def relu2_ep(base):
                def ep(j, t0, w, bank, mmtok):
                    k = rr["n"]
                    rr["n"] += 1
                    a1 = P.op("scalar", lambda e: e.activation(out=tmpf[k % 2][:, 0:w], in_=bank, func=AF.Relu),
                              waits=[mmtok, rr.get(k - 2)], inc=s_ev)
                    d1 = P.op("vector", lambda e: e.tensor_tensor(out=B3[:, j - base, t0:t0 + w], in0=tmpf[k % 2][:, 0:w],
                                                                  in1=tmpf[k % 2][:, 0:w], op=ALU.mult),
                              waits=[a1], inc=s_dve)
                    rr[k] = d1
                    return a1
                return ep

            for gq in range(4):
                gemm(w1_d[i], list(range(gq * 16, (gq + 1) * 16)), 32, actA, T, relu2_ep(gq * 32), wname=f"w1_{i}")
                P.barrier()
                gemm(w2_d[i][gq * 16 * 128:(gq + 1) * 16 * 128, :], list(range(16)), 32, actB, T,
                     resid_epilogue(hd_tile, hd_tile), wname=f"w2_{i}", pbase=gq)
                P.barrier()

        def attn_layer():
            Ct = Bf[0:32, 0:4096]
            Sn = Bf[0:32, 4096:8192]
            ang = Bf[0:32, 8192:12288]
            posi = Bf.bitcast(I32)[0:32, 12288:16384]
            pt = P.op("sync", lambda e: e.dma_start(out=posi, in_=bass.AP(pos_d.tensor, 0, [[0, 32], [1, SEQ]])), inc=s_misc, dma=True)
            d = P.op("vector", lambda e: e.tensor_copy(out=ang, in_=posi), waits=[pt], inc=s_dve)
            d = P.op("vector", lambda e: e.tensor_scalar(out=ang, in0=ang, scalar1=small[0:32, 2:3], scalar2=None, op0=ALU.mult),
                     waits=[d], inc=s_dve)
            SC = 6.28315
            d1 = P.op("vector", lambda e: e.tensor_scalar(out=Ct, in0=ang, scalar1=1.0 / (2 * math.pi), scalar2=0.25,
                                                          op0=ALU.mult, op1=ALU.add), waits=[d], inc=s_dve)
            d1 = P.op("vector", lambda e: e.tensor_copy(out=posi, in_=Ct), waits=[d1], inc=s_dve)
            d1 = P.op("vector", lambda e: e.tensor_copy(out=Sn, in_=posi), waits=[d1], inc=s_dve)
            d1 = P.op("vector", lambda e: e.tensor_tensor(out=Ct, in0=Ct, in1=Sn, op=ALU.subtract), waits=[d1], inc=s_dve)
            a1 = P.op("scalar", lambda e: e.activation(out=Ct, in_=Ct, func=AF.Sin, scale=SC), waits=[d1], inc=s_act)
            d2 = P.op("vector", lambda e: e.tensor_scalar(out=Sn, in0=ang, scalar1=1.0 / (2 * math.pi), scalar2=None,
                                                          op0=ALU.mult), waits=[d1], inc=s_dve)
            d2 = P.op("vector", lambda e: e.tensor_copy(out=posi, in_=Sn), waits=[d2], inc=s_dve)
            d2 = P.op("vector", lambda e: e.tensor_copy(out=ang, in_=posi), waits=[d2], inc=s_dve)
            d2 = P.op("vector", lambda e: e.tensor_tensor(out=Sn, in0=Sn, in1=ang, op=ALU.subtract), waits=[d2], inc=s_dve)
            a2 = P.op("scalar", lambda e: e.activation(out=Sn, in_=Sn, func=AF.Sin, scale=SC), waits=[d2, a1], inc=s_act)
            P.op("vector", lambda e: e.tensor_scalar(out=Sn, in0=Sn, scalar1=small[0:32, 3:4], scalar2=None, op0=ALU.mult),
                 waits=[a2], inc=s_dve)
            lv = small[:, 8:8 + 512]
            d = P.op("vector", lambda e: e.tensor_tensor(out=tmpf[0][:, 0:128], in0=lv[:, 0:128], in1=lv[:, 128:256], op=ALU.mult),
                     waits=[(s_misc, s_misc.n)], inc=s_dve)
            d = P.op("vector", lambda e: e.tensor_tensor(out=tmpf[0][:, 128:256], in0=lv[:, 256:384], in1=lv[:, 384:512], op=ALU.mult),
                     waits=[d], inc=s_dve)
            d = P.op("vector", lambda e: e.reduce_sum(out=scal[:, 1:2], in_=tmpf[0][:, 0:128], axis=AX.X), waits=[d], inc=s_dve)
            d = P.op("vector", lambda e: e.reduce_sum(out=scal[:, 2:3], in_=tmpf[0][:, 128:256], axis=AX.X), waits=[d], inc=s_dve)
            a = P.op("scalar", lambda e: e.activation(out=scal[:, 3:5], in_=scal[:, 1:3], func=AF.Exp), waits=[d], inc=s_act)
            d = P.op("vector", lambda e: e.tensor_tensor(out=scal[:, 5:6], in0=scal[:, 4:5], in1=scal[:, 3:4], op=ALU.subtract),
                     waits=[a], inc=s_dve)
            P.op("vector", lambda e: e.tensor_scalar(out=scal[:, 0:1], in0=scal[:, 5:6], scalar1=-lambda_init(0), scalar2=None,
                                                     op0=ALU.add), waits=[d], inc=s_dve)
            P.barrier()
            Pm = small[0:32, 520:552]
            qf = [tmpf[0], tmpf[1]]
            qs = {"n": 0, "pend": None}

            def qk_ep(tt):
                def ep(j, t0, w, bank, mmtok):
                    k = qs["n"]
                    qs["n"] += 1
                    tok0 = tt * T + t0
                    if j >= 64:
                        raise AssertionError
                    a1 = P.op("scalar", lambda e: e.activation(out=qf[k % 2][0:32, 0:w], in_=bank[0:32, :], func=AF.Identity),
                              waits=[mmtok, qs.get(("d", k - 2)), qs.get(("st", k - 2))], inc=s_ev)
                    a2 = P.op("scalar", lambda e: e.activation(out=ob[k % 2][:, 0:w], in_=bank, func=AF.Identity),
                              waits=[a1], inc=s_ev)
                    sw = ps[6 + k % 2][0:32, 0:w]
                    p1 = P.op("tensor", lambda e: e.matmul(sw, Pm, qf[k % 2][0:32, 0:w], start=True, stop=True),
                              waits=[a1, qs.get(("d", k - 2))], inc=s_pe2)
                    d1 = P.op("vector", lambda e: e.tensor_tensor(out=ht[2 + k % 2][0:32, 0:w], in0=sw, in1=Sn[:, tok0:tok0 + w], op=ALU.mult),
                              waits=[p1], inc=s_dve)
                    d2 = P.op("vector", lambda e: e.tensor_tensor(out=qf[k % 2][0:32, 0:w], in0=qf[k % 2][0:32, 0:w],
                                                                  in1=Ct[:, tok0:tok0 + w], op=ALU.mult), waits=[d1], inc=s_dve)
                    d3 = P.op("vector", lambda e: e.tensor_tensor(out=ob[k % 2][0:32, 0:w], in0=qf[k % 2][0:32, 0:w],
                                                                  in1=ht[2 + k % 2][0:32, 0:w], op=ALU.add), waits=[d2, a2], inc=s_dve)
                    qs[("d", k)] = d3
                    if j < 32:
                        dst = Qt_d[j * 128:(j + 1) * 128, t0:t0 + w]
                    else:
                        dst = Kt_d[(j - 32) * 128:(j - 31) * 128, tok0:tok0 + w]
                    qs[("st", k)] = P.op("sync", lambda e: e.dma_start(out=dst, in_=ob[k % 2][:, 0:w]), waits=[d3, a2],
                                         inc=s_st[k % 2], dma=True)
                    return a2
                return ep

            vs = {"n": 0}

            def v_ep(tt):
                def ep(jg, t0, w, bank, mmtok):
                    k = vs["n"]
                    vs["n"] += 1
                    f0 = (jg - 32) * NG
                    d1 = P.op("vector", lambda e: e.tensor_copy(out=ob[k % 2][:, 0:NG], in_=bank),
                              waits=[mmtok, vs.get(k - 2)], inc=s_ev)
                    vs[k] = P.op("sync", lambda e: e.dma_start(out=V_d[tt * T + t0: tt * T + t0 + 128, f0:f0 + NG],
                                                               in_=ob[k % 2][:, 0:NG]), waits=[d1], inc=s_st[2 + k % 2], dma=True)
                    return d1
                return ep

            for tt in range(4):
                norm_phase(lambda ch, t0, w, tt=tt: xseq[ch * 128:(ch + 1) * 128, tt * T + t0: tt * T + t0 + w], T, "g_attn", actA)
                P.barrier()
                qs["n"] = 0
                for kk in [k_ for k_ in list(qs.keys()) if isinstance(k_, tuple)]:
                    del qs[kk]
                gemm(wqkv_d, list(range(0 if tt == 0 else 16, 32)), 32, actA, T, qk_ep(tt), wname="wqkv")
                P.barrier()
                vs_keys = [k_ for k_ in vs if k_ != "n"]
                for kk in vs_keys:
                    del vs[kk]
                vs["n"] = 0
                gemm(wqkv_d, list(range(32, 48)), 32, actA, T, v_ep(tt), moving=True, wname="wqkv")
                P.barrier()
            if DEBUG["stop"] == "qkv":
                return

            HB = 18432
            def KT(buf, m):
                return ARENA[:, buf * HB + m * 4096: buf * HB + (m + 1) * 4096]
            def VT(buf):
                return ARENA[:, buf * HB + 8192: buf * HB + 16384].rearrange("p (k f) -> p k f", k=32)
            def QT(buf, m):
                return ARENA[:, buf * HB + 16384 + m * 1024: buf * HB + 16384 + (m + 1) * 1024]
            PTb = 2 * HB
            def PTt(g):
                return ARENA[:, PTb + (g % 4) * 512: PTb + (g % 4 + 1) * 512]
            O1 = Bf[:, 0:1024].rearrange("p (v t) -> p v t", v=2)
            OS = [Bf[:, 1024 + i * 1024: 2048 + i * 1024].rearrange("p (v t) -> p v t", v=2) for i in range(2)]
            rden = [Bf[:, 3072:3584], Bf[:, 3584:4096]]
            tmpo = Bf[:, 4096:5120].rearrange("p (v t) -> p v t", v=2)
            hl = {}
            s_hk = s_x[0:2]

            def load_head(h):
                buf = h % 2
                w = [hl.get(("free", h - 2))]
                toks = []
                for m in range(2):
                    toks.append(P.op("sync", lambda e, m=m: e.dma_start(out=KT(buf, m), in_=Kt_d[(2 * h + m) * 128:(2 * h + m + 1) * 128, :]),
                                     waits=w, inc=s_hk[buf], dma=True))
                    toks.append(P.op("sync", lambda e, m=m: e.dma_start(out=QT(buf, m), in_=Qt_d[(2 * h + m) * 128:(2 * h + m + 1) * 128, :]),
                                     waits=w, inc=s_hk[buf], dma=True))
                toks.append(P.op("sync", lambda e: e.dma_start(
                    out=VT(buf), in_=V_d[:, h * 256:(h + 1) * 256].rearrange("(k p) f -> p k f", p=128)),
                    waits=w, inc=s_hk[buf], dma=True))
                hl[("ld", h)] = toks[-1]

            s_sc, s_exp, s_pv, s_aep = s_x[2], s_x[3], s_x[4], s_x[5]
            gsc = {"g": 0, "u": 0}
            exp_tok = {}
            pv_tok = {}
            ep_tok = {}
            os_st = {}
            load_head(0)
            for h in range(16):
                if h + 1 < 16:
                    load_head(h + 1)
                buf = h % 2
                ldtok = hl[("ld", h)]
                for qh in range(2):
                    for m in range(2):
                        u = gsc["u"]
                        gsc["u"] += 1
                        ab = 2 + 3 * (u % 2)
                        accfree = ep_tok.get(u - 2)
                        g0 = gsc["g"]

                        def S_mm(kc):
                            g = g0 + kc
                            return P.op("tensor", lambda e: e.matmul(ps[g % 2][:], KT(buf, m)[:, kc * 128:(kc + 1) * 128],
                                                                     QT(buf, m)[:, qh * 512:(qh + 1) * 512], start=True, stop=True),
                                        waits=[ldtok, exp_tok.get(g - 2)], inc=s_sc)

                        def EXP(kc, sctok):
                            g = g0 + kc
                            exp_tok[g] = P.op("scalar", lambda e: e.activation(out=PTt(g), in_=ps[g % 2][:], func=AF.Exp,
                                                                               scale=128.0 ** -0.5),
                                              waits=[sctok, pv_tok.get(g - 4)], inc=s_exp)

                        def PV(kc):
                            g = g0 + kc
                            w = [exp_tok[g]] + ([accfree] if kc == 0 else [])
                            P.op("tensor", lambda e: e.matmul(ps[ab + 2][:], ones[:], PTt(g), start=(kc == 0), stop=(kc == 31)), waits=w)
                            P.op("tensor", lambda e: e.matmul(ps[ab][:], VT(buf)[:, kc, 0:128], PTt(g), start=(kc == 0), stop=(kc == 31)))
                            pv_tok[g] = P.op("tensor", lambda e: e.matmul(ps[ab + 1][:], VT(buf)[:, kc, 128:256], PTt(g),
                                                                          start=(kc == 0), stop=(kc == 31)), inc=s_pv)

                        EXP(0, S_mm(0))
                        EXP(1, S_mm(1))
                        for kc in range(32):
                            PV(kc)
                            if kc + 2 < 32:
                                EXP(kc + 2, S_mm(kc + 2))
                        gsc["g"] += 32
                        lastpv = pv_tok[g0 + 31]
                        d = P.op("vector", lambda e: e.reciprocal(out=rden[u % 2], in_=ps[ab + 2][:]), waits=[lastpv], inc=s_dve)
                        if m == 0:
                            d = P.op("vector", lambda e: e.tensor_tensor(out=O1[:, 0, :], in0=ps[ab][:], in1=rden[u % 2], op=ALU.mult),
                                     waits=[d, os_st.get("o1")], inc=s_dve)
                            d = P.op("vector", lambda e: e.tensor_tensor(out=O1[:, 1, :], in0=ps[ab + 1][:], in1=rden[u % 2], op=ALU.mult),
                                     waits=[d], inc=s_aep)
                            ep_tok[u] = d
                        else:
                            osb = OS[(u // 2) % 2]
                            d = P.op("vector", lambda e: e.tensor_tensor(out=tmpo[:, 0, :], in0=ps[ab][:], in1=rden[u % 2], op=ALU.mult),
                                     waits=[d], inc=s_dve)
                            d = P.op("vector", lambda e: e.tensor_tensor(out=tmpo[:, 1, :], in0=ps[ab + 1][:], in1=rden[u % 2], op=ALU.mult),
                                     waits=[d], inc=s_aep)
                            ep_tok[u] = d
                            for vc in range(2):
                                d = P.op("vector", lambda e, vc=vc: e.scalar_tensor_tensor(
                                    out=osb[:, vc, :], in0=tmpo[:, vc, :], scalar=scal[:, 0:1], in1=O1[:, vc, :],
                                    op0=ALU.mult, op1=ALU.add), waits=[d, os_st.get((u // 2) % 2)], inc=s_dve)
                            os_st["o1"] = d
                            os_st[(u // 2) % 2] = P.op("sync", lambda e: e.dma_start(
                                out=Op_d[h * 256:(h + 1) * 256, qh * 512:(qh + 1) * 512].rearrange("(v p) t -> p v t", p=128),
                                in_=osb), waits=[d], inc=s_st[(u // 2) % 2], dma=True)
                hl[("free", h)] = pv_tok[gsc["g"] - 1]
            P.barrier()
            if DEBUG["stop"] == "attn":
                return
            C.snap("Op", Op_d[:, 0:128])
            norm_phase(lambda ch, t0, w: Op_d[ch * 128:(ch + 1) * 128, t0:t0 + w], T,
                       lambda c: small[:, c:c + 1], actB, ngroups=16, cpg=2,
                       scale=1.0 / (256 * 0.8 * 0.8), eps_col=1)
            P.barrier()
            gemm(wo_d, list(range(16)), 32, actB, T,
                 resid_epilogue(lambda j, t0, w: xseq[j * 128:(j + 1) * 128, t0:t0 + w], hd_tile), wname="wo")
            P.barrier()

        def rnn_layer():
            norm_phase(lambda ch, t0, w: h_d[ch * 128:(ch + 1) * 128, t0:t0 + w], T, "g_rnn", actA)
            P.barrier()
            halo = Bf[:, 0:96].rearrange("p (c x) -> p c x", c=32)
            us = {"n": 0}

            def u_ep(j, t0, w, bank, mmtok):
                k = us["n"]
                us["n"] += 1
                slot = k % 4
                w_ = [mmtok]
                if st["stv"][slot] is not None:
                    w_.append(st["stv"][slot])
                a1 = P.op("scalar", lambda e: e.activation(out=ht[slot][:, 0:w], in_=bank, func=AF.Identity), waits=w_, inc=s_ev)
                if t0 == 0:
                    P.op("vector", lambda e: e.tensor_copy(out=halo[:, j, 0:1], in_=ht[slot][:, 0:1]), waits=[a1], inc=s_dve)
                else:
                    P.op("vector", lambda e: e.tensor_copy(out=halo[:, j, 1:3], in_=ht[slot][:, w - 2:w]), waits=[a1], inc=s_dve)
                store_tile(ht[slot][:, 0:w], u_d[j * 128:(j + 1) * 128, t0:t0 + w], slot, [a1, (s_dve, s_dve.n)])
                return a1

            def gate_ep(j, t0, w, bank, mmtok):
                k = us["n"]
                us["n"] += 1
                slot = k % 4
                w_ = [mmtok]
                if st["stv"][slot] is not None:
                    w_.append(st["stv"][slot])
                x = ht[slot][:, 0:w]
                t = tmpf[k % 2][:, 0:w]
                a1 = P.op("scalar", lambda e: e.activation(out=x, in_=bank, func=AF.Identity), waits=w_, inc=s_ev)
                d = P.op("vector", lambda e: e.tensor_tensor(out=t, in0=x, in1=x, op=ALU.mult), waits=[a1, us.get(("t", k - 2))], inc=s_dve)
                d = P.op("vector", lambda e: e.tensor_scalar(out=t, in0=t, scalar1=0.044715, scalar2=1.0, op0=ALU.mult, op1=ALU.add),
                         waits=[d], inc=s_dve)
                d = P.op("vector", lambda e: e.tensor_tensor(out=t, in0=t, in1=x, op=ALU.mult), waits=[d], inc=s_dve)
                a2 = P.op("scalar", lambda e: e.activation(out=t, in_=t, func=AF.Sigmoid, scale=1.5957691216057308), waits=[d], inc=s_act)
                d = P.op("vector", lambda e: e.tensor_tensor(out=x, in0=x, in1=t, op=ALU.mult), waits=[a2], inc=s_dve)
                us[("t", k)] = d
                store_tile(x, gg_d[(j - 32) * 128:(j - 31) * 128, t0:t0 + w], slot, [d])
                return a1

            gemm(win_d, list(range(0, 16)), 32, actA, T, u_ep, wname="win")
            gemm(win_d, list(range(16, 32)), 32, actA, T, gate_ep, wname="win")
            P.barrier()
            C.snap("u", u_d[:, 0:128])
            C.snap("gg", gg_d[:, 0:128])
            g1 = P.op("gpsimd", lambda e: e.dma_start(out=halo_in, in_=Bf[:, 0:96]), inc=s_g, dma=True)
            c1 = P.op("gpsimd", lambda e: e.collective_compute("AllGather", ALU.bypass, replica_groups=[list(range(8))],
                                                                ins=[halo_in.opt()], outs=[halo_all.opt()]), waits=[g1], inc=s_cc)
            hall = Bf[:, 128:128 + 8 * 96].rearrange("p (r x) -> p r x", r=8)
            g2 = P.op("gpsimd", lambda e: e.dma_start(out=hall, in_=halo_all.rearrange("(r p) x -> p r x", p=128)), waits=[c1],
                      inc=s_g, dma=True)
            hp = Bf[:, 1024:1120].rearrange("p (c x) -> p c x", c=32)
            d = P.op("vector", lambda e: e.memset(Bf[:, 1024:1120], 0.0), waits=[g2], inc=s_dve)
            for r in range(8):
                hr = hall[:, r, :].rearrange("p (c x) -> p c x", c=32)
                d = P.op("vector", lambda e, hr=hr, r=r: e.scalar_tensor_tensor(
                    out=hp[:, :, 0:2], in0=hr[:, :, 1:3], scalar=masks[:, 8 + r:9 + r], in1=hp[:, :, 0:2],
                    op0=ALU.mult, op1=ALU.add), waits=[d], inc=s_dve)
                d = P.op("vector", lambda e, hr=hr, r=r: e.scalar_tensor_tensor(
                    out=hp[:, :, 2:3], in0=hr[:, :, 0:1], scalar=masks[:, 16 + r:17 + r], in1=hp[:, :, 2:3],
                    op0=ALU.mult, op1=ALU.add), waits=[d], inc=s_dve)
            cneg = Bf[:, 1152:1216]
            a = P.op("scalar", lambda e: e.activation(out=cneg, in_=vecs[:, VEC["lam_f"] * 32:(VEC["lam_f"] + 2) * 32], func=AF.Exp, scale=-1.0),
                     waits=[d], inc=s_act)
            a = P.op("scalar", lambda e: e.activation(out=cneg, in_=cneg, func=AF.Ln, bias=epst[:, 3:4]), waits=[a], inc=s_act)
            d = P.op("vector", lambda e: e.tensor_scalar(out=cneg, in0=cneg, scalar1=-8.0, scalar2=None, op0=ALU.mult), waits=[a], inc=s_dve)
            P.barrier()
            Af = ARENA.bitcast(F32)
            up = Af[:, 0:2 * 1027].rearrange("p (c t) -> p c t", c=2)
            uc = Af[:, 2056:2056 + 2048].rearrange("p (c t) -> p c t", c=2)
            ucb = ARENA[:, 8208 + 0: 8208 + 2048].rearrange("p (c t) -> p c t", c=2)
            wgt = ARENA[:, 10256:10256 + 2048].rearrange("p (m i n) -> p m i n", m=4, i=2)
            gt = [Af[:, 6400 + i * 1024: 6400 + (i + 1) * 1024] for i in range(8)]
            car = Bf[:, 2048:2176].rearrange("p (x c) -> p x c", x=4)
            zer = Af[:, 14600:15624]
            d0 = P.op("vector", lambda e: e.memset(zer, 0.0), inc=s_dve)
            s_u, s_w4 = s_x[0], s_x[1]
            prev = {"d": d0, "st": None}

            def rev(ap_t, n=1024):
                return bass.AP(ap_t.tensor, ap_t.offset + (n - 1), [list(ap_t.ap[0]), [-1, n]])

            for nb in range(16):
                lu = P.op("sync", lambda e, nb=nb: e.dma_start(
                    out=up[:, :, 2:1026], in_=u_d[nb * 256:(nb + 1) * 256, :].rearrange("(c p) t -> p c t", p=128)),
                    waits=[prev["d"]], inc=s_u, dma=True)
                lw = P.op("gpsimd", lambda e, nb=nb: e.dma_start(
                    out=ARENA[:, 10256:10256 + 2048], in_=wg_d[nb * 128:(nb + 1) * 128, :]), waits=[prev["d"], PTOK[("wg", 0)]], inc=s_w4, dma=True)
                d = P.op("vector", lambda e, nb=nb: e.tensor_copy(out=up[:, :, 0:2], in_=hp[:, 2 * nb:2 * nb + 2, 0:2]),
                         waits=[prev["d"]], inc=s_dve)
                d = P.op("vector", lambda e, nb=nb: e.tensor_copy(out=up[:, :, 1026:1027], in_=hp[:, 2 * nb:2 * nb + 2, 2:3]),
                         waits=[d], inc=s_dve)
                for cc in range(2):
                    ch = 2 * nb + cc
                    d = P.op("vector", lambda e, cc=cc, ch=ch: e.tensor_scalar(
                        out=uc[:, cc, :], in0=up[:, cc, 0:1024], scalar1=vcol("cw0", ch), scalar2=vcol("cb", ch),
                        op0=ALU.mult, op1=ALU.add), waits=[d, lu], inc=s_dve)
                    for tau in range(1, 4):
                        d = P.op("vector", lambda e, cc=cc, ch=ch, tau=tau: e.scalar_tensor_tensor(
                            out=uc[:, cc, :], in0=up[:, cc, tau:tau + 1024], scalar=vcol(f"cw{tau}", ch), in1=uc[:, cc, :],
                            op0=ALU.mult, op1=ALU.add), waits=[d], inc=s_dve)
                    d = P.op("vector", lambda e, cc=cc: e.tensor_copy(out=ucb[:, cc, :], in_=uc[:, cc, :]), waits=[d], inc=s_dve)
                ucd = d
                bias_names = ["ba_f", "bi_f", "ba_b", "bi_b"]
                for oc in range(2):
                    ch = 2 * nb + oc
                    gts = []
                    for mat in range(4):
                        for th in range(2):
                            bk = ps[(mat * 2 + th) % 8][:]
                            for ic in range(2):
                                p = P.op("tensor", lambda e, bk=bk, mat=mat, ic=ic, oc=oc, th=th: e.matmul(
                                    bk, wgt[:, mat, ic, oc * 128:(oc + 1) * 128], ucb[:, ic, th * 512:(th + 1) * 512],
                                    start=(ic == 0), stop=(ic == 1)),
                                    waits=([ucd, lw, (s_act, s_act.n)] if ic == 0 else []), inc=(s_mm if ic == 1 else None))
                            g = gt[mat][:, th * 512:(th + 1) * 512]
                            a = P.op("scalar", lambda e, bk=bk, g=g, mat=mat, ch=ch: e.activation(
                                out=g, in_=bk, func=AF.Sigmoid, bias=vcol(bias_names[mat], ch)),
                                waits=[p, prev["d"]], inc=s_act)
                    for di in range(2):
                        r_t, i_t = gt[2 * di], gt[2 * di + 1]
                        a_t, b_t = gt[4 + 2 * di], gt[5 + 2 * di]
                        cn = cneg[:, di * 32 + ch: di * 32 + ch + 1]
                        a1 = P.op("scalar", lambda e, a_t=a_t, r_t=r_t, cn=cn: e.activation(out=a_t, in_=r_t, func=AF.Exp, scale=cn),
                                  waits=[(s_act, s_act.n), prev["st"]], inc=s_act)
                        d = P.op("vector", lambda e, i_t=i_t, oc=oc: e.tensor_tensor(out=i_t, in0=i_t, in1=uc[:, oc, :], op=ALU.mult),
                                 waits=[(s_act, s_act.n)], inc=s_dve)
                        d = P.op("vector", lambda e, r_t=r_t, a_t=a_t: e.tensor_tensor(out=r_t, in0=a_t, in1=a_t, op=ALU.mult),
                                 waits=[d, a1], inc=s_dve)
                        a2 = P.op("scalar", lambda e, r_t=r_t: e.activation(out=r_t, in_=r_t, func=AF.Sqrt, scale=-1.0, bias=epst[:, 3:4]),
                                  waits=[d], inc=s_act)
                        d = P.op("vector", lambda e, b_t=b_t, r_t=r_t, i_t=i_t: e.tensor_tensor(out=b_t, in0=r_t, in1=i_t, op=ALU.mult),
                                 waits=[a2], inc=s_dve)
                        s1 = P.op("sync", lambda e, a_t=a_t, di=di, ch=ch: e.dma_start(out=ab_d[2 * di][ch * 128:(ch + 1) * 128, :], in_=a_t),
                                  waits=[d], inc=s_st[0], dma=True)
                        s2 = P.op("sync", lambda e, b_t=b_t, di=di, ch=ch: e.dma_start(out=ab_d[2 * di + 1][ch * 128:(ch + 1) * 128, :], in_=b_t),
                                  waits=[d], inc=s_st[1], dma=True)
                        A_in, B_in = (a_t, b_t) if di == 0 else (rev(a_t), rev(b_t))
                        d = P.op("vector", lambda e, r_t=r_t, A_in=A_in, B_in=B_in: e.tensor_tensor_scan(
                            out=r_t, data0=A_in, data1=B_in, initial=0.0, op0=ALU.mult, op1=ALU.add), waits=[d], inc=s_dve)
                        d = P.op("vector", lambda e, di=di, ch=ch, r_t=r_t: e.tensor_copy(out=car[:, 2 * di, ch:ch + 1], in_=r_t[:, 1023:1024]),
                                 waits=[d], inc=s_dve)
                        d = P.op("vector", lambda e, i_t=i_t, A_in=A_in: e.tensor_tensor_scan(
                            out=i_t, data0=A_in, data1=zer, initial=1.0, op0=ALU.mult, op1=ALU.add), waits=[d], inc=s_dve)
                        d = P.op("vector", lambda e, di=di, ch=ch, i_t=i_t: e.tensor_copy(out=car[:, 2 * di + 1, ch:ch + 1], in_=i_t[:, 1023:1024]),
                                 waits=[d, s1, s2], inc=s_dve)
                        prev["d"] = d
                        prev["st"] = s2
            P.barrier()
            if DEBUG["stop"] == "rnnA":
                return
            for i_ in range(4):
                C.snap(f"ab{i_}", ab_d[i_][:, 0:128])
            g1 = P.op("gpsimd", lambda e: e.dma_start(out=car_in, in_=Bf[:, 2048:2176]), inc=s_g, dma=True)
            c1 = P.op("gpsimd", lambda e: e.collective_compute("AllGather", ALU.bypass, replica_groups=[list(range(8))],
                                                                ins=[car_in.opt()], outs=[car_all.opt()]), waits=[g1], inc=s_cc)
            call = Bf[:, 2304:2304 + 1024].rearrange("p (r x c) -> p r x c", r=8, x=4)
            g2 = P.op("gpsimd", lambda e: e.dma_start(out=Bf[:, 2304:2304 + 1024].rearrange("p (r f) -> p r f", r=8),
                                                      in_=car_all.rearrange("(r p) f -> p r f", p=128)), waits=[c1], inc=s_g, dma=True)
            E = carr[:, 0:32]
            Cf = carr[:, 32:64]
            Cb = carr[:, 64:96]
            d = P.op("vector", lambda e: e.memset(carr[:], 0.0), waits=[g2], inc=s_dve)
            for r in range(8):
                if r == 4:
                    d = P.op("vector", lambda e: e.memset(E, 0.0), waits=[d], inc=s_dve)
                d = P.op("vector", lambda e, r=r: e.scalar_tensor_tensor(out=Cf, in0=E, scalar=masks[:, r:r + 1], in1=Cf,
                                                                          op0=ALU.mult, op1=ALU.add), waits=[d], inc=s_dve)
                d = P.op("vector", lambda e, r=r: e.tensor_tensor(out=E, in0=E, in1=call[:, r, 1, :], op=ALU.mult), waits=[d], inc=s_dve)
                d = P.op("vector", lambda e, r=r: e.tensor_tensor(out=E, in0=E, in1=call[:, r, 0, :], op=ALU.add), waits=[d], inc=s_dve)
            d = P.op("vector", lambda e: e.memset(E, 0.0), waits=[d], inc=s_dve)
            for r in range(7, -1, -1):
                if r == 3:
                    d = P.op("vector", lambda e: e.memset(E, 0.0), waits=[d], inc=s_dve)
                d = P.op("vector", lambda e, r=r: e.scalar_tensor_tensor(out=Cb, in0=E, scalar=masks[:, r:r + 1], in1=Cb,
                                                                          op0=ALU.mult, op1=ALU.add), waits=[d], inc=s_dve)
                d = P.op("vector", lambda e, r=r: e.tensor_tensor(out=E, in0=E, in1=call[:, r, 3, :], op=ALU.mult), waits=[d], inc=s_dve)
                d = P.op("vector", lambda e, r=r: e.tensor_tensor(out=E, in0=E, in1=call[:, r, 2, :], op=ALU.add), waits=[d], inc=s_dve)
            P.barrier()
            wt_ = [Af[:, i * 1024:(i + 1) * 1024] for i in range(12)]
            pb = {"d": None}
            for ch in range(32):
                o = (ch % 2) * 6
                ta, tb_, tc, td, tg, tf_ = wt_[o], wt_[o + 1], wt_[o + 2], wt_[o + 3], wt_[o + 4], wt_[o + 5]
                w0 = [pb.get(ch - 2)]
                l = []
                sset = (s_ld if ch % 2 == 0 else s_st)
                for i_, tl in enumerate([ta, tb_, tc, td]):
                    l.append(P.op("sync", lambda e, i_=i_, tl=tl, ch=ch: e.dma_start(out=tl, in_=ab_d[i_][ch * 128:(ch + 1) * 128, :]),
                                  waits=w0, inc=sset[i_], dma=True))
                lg = P.op("sync", lambda e, tg=tg, ch=ch: e.dma_start(out=tg, in_=gg_d[ch * 128:(ch + 1) * 128, :]), waits=w0,
                          inc=s_x[ch % 2], dma=True)
                d = P.op("vector", lambda e, ta=ta, tb_=tb_, ch=ch: e.tensor_tensor_scan(
                    out=tf_, data0=ta, data1=tb_, initial=Cf[:, ch:ch + 1], op0=ALU.mult, op1=ALU.add), waits=[l[0], l[1]], inc=s_dve)
                d = P.op("vector", lambda e, tc=tc, td=td, ch=ch: e.tensor_tensor_scan(
                    out=rev(ta), data0=rev(tc), data1=rev(td), initial=Cb[:, ch:ch + 1], op0=ALU.mult, op1=ALU.add),
                    waits=[l[2], l[3], d], inc=s_dve)
                d = P.op("vector", lambda e, ta=ta, tb_=tb_: e.tensor_tensor(out=ta, in0=ta, in1=tf_, op=ALU.add), waits=[d], inc=s_dve)
                d = P.op("vector", lambda e, ta=ta, tg=tg, ch=ch: e.tensor_tensor(out=B3[:, ch, :], in0=ta, in1=tg, op=ALU.mult),
                         waits=[d, lg], inc=s_dve)
                pb[ch] = d
            P.barrier()
            gemm(wout_d, list(range(16)), 32, actB, T, resid_epilogue(hd_tile, hd_tile), wname="wout")
            P.barrier()

        snaps = {"n": 0}

        def snap(name, src_ap, rows=D, dt=F32):
            if not DEBUG["snap"]:
                return
            t = nc.dram_tensor("snap_" + name, [rows, 128], dt, kind="ExternalOutput").ap()
            P.op("sync", lambda e: e.dma_start(out=t, in_=src_ap), inc=s_ld[snaps["n"] % 4], dma=True)
            snaps["n"] += 1
            P.barrier()
        C.snap = snap
        attn_layer()
        snap("h_attn", h_d[:, 0:128])
        if DEBUG["stop"] is None or DEBUG["stop"] in ("l0", "all", "rnnA", "l1"):
            xattn_block(0)
            snap("h_x0", h_d[:, 0:128])
            mlp_block(0)
            snap("h_m0", h_d[:, 0:128])
        if DEBUG["stop"] in (None, "all", "rnnA", "l1"):
            rnn_layer()
            snap("h_rnn", h_d[:, 0:128])
        if DEBUG["stop"] in (None, "all"):
            xattn_block(1)
            snap("h_x1", h_d[:, 0:128])
            mlp_block(1)
            snap("h_m1", h_d[:, 0:128])
        if DEBUG["stop"] in (None, "all"):
            norm_phase(lambda ch, t0, w: h_d[ch * 128:(ch + 1) * 128, t0:t0 + w], T, "g_fin", None,
                       store_fn=lambda ch, t0, w: out_d[ch * 128:(ch + 1) * 128, t0:t0 + w])
        else:
            P.op("sync", lambda e: e.dma_start(out=out_d[0:128, 0:512], in_=ht[0][:]), inc=s_st[0], dma=True)
        P.barrier()
        P.emit()
    return nc


def tile_w(W):
    Kd, N = W.shape
    KC, NJ = Kd // 128, N // NG
    return np.ascontiguousarray(W.reshape(KC, 128, NJ, NG).transpose(2, 1, 0, 3)).reshape(NJ * 128, KC * NG)


def vec32(v):
    return np.ascontiguousarray(np.asarray(v, np.float32).reshape(32, 128).T)


def prep_inputs(inp):
    f = lambda a: np.asarray(a, dtype=np.float32)
    x = f(inp["x"]); mem = f(inp["mem"]); pos = np.asarray(inp["positions"]).astype(np.int32)
    shared = {}
    shared["wqkv"] = tile_w(f(inp["attn_w_qkv"])[0])
    shared["wo"] = tile_w(f(inp["attn_w_o"])[0])
    shared["win"] = tile_w(f(inp["rnn_w_in"])[0])
    shared["wout"] = tile_w(f(inp["rnn_w_out"])[0])
    for i in range(2):
        shared[f"xq{i}"] = tile_w(f(inp["xattn_w_q"])[i])
        shared[f"xkv{i}"] = tile_w(f(inp["xattn_w_kv"])[i])
        shared[f"xo{i}"] = tile_w(f(inp["xattn_w_o"])[i])
        shared[f"w1_{i}"] = tile_w(f(inp["mlp_w1"])[i])
        w2 = f(inp["mlp_w2"])[i]
        shared[f"w2_{i}"] = np.concatenate([tile_w(w2[g * 4096:(g + 1) * 4096]) for g in range(4)], axis=0)
    mats = [f(inp[n])[0] for n in ["rnn_wa_f", "rnn_wi_f", "rnn_wa_b", "rnn_wi_b"]]
    wg = np.stack(mats, axis=1)
    wg = wg.reshape(16, 4, 2, 128, 256).transpose(0, 3, 1, 2, 4)
    shared["wg"] = np.ascontiguousarray(wg).reshape(16 * 128, 4 * 2 * 256)
    vn = [inp["attn_norm_g"][0], inp["rnn_norm_g"][0], inp["xattn_norm_g"][0], inp["xattn_norm_g"][1],
          inp["xattn_mem_g"][0], inp["xattn_mem_g"][1], inp["mlp_norm_g"][0], inp["mlp_norm_g"][1], inp["final_g"],
          inp["rnn_conv_w"][0][0], inp["rnn_conv_w"][0][1], inp["rnn_conv_w"][0][2], inp["rnn_conv_w"][0][3],
          inp["rnn_conv_b"][0], inp["rnn_lam_f"][0], inp["rnn_lam_b"][0],
          np.asarray(inp["rnn_ba_f"][0]).reshape(-1), np.asarray(inp["rnn_bi_f"][0]).reshape(-1),
          np.asarray(inp["rnn_ba_b"][0]).reshape(-1), np.asarray(inp["rnn_bi_b"][0]).reshape(-1)]
    shared["vecs"] = np.ascontiguousarray(np.concatenate([vec32(v) for v in vn], axis=1))
    small = np.zeros((128, 8 + 512 + 32), np.float32)
    small[:, 0:2] = f(inp["attn_subln_g"])[0].reshape(2, 128).T
    invf = (500000.0 ** (-np.arange(0, 32, 2, dtype=np.float32) / 32)).astype(np.float32)
    small[0:32, 2] = np.concatenate([invf, invf])
    small[0:16, 3] = -1.0
    small[16:32, 3] = 1.0
    lamv = np.concatenate([f(inp["attn_lambda_q1"])[0], f(inp["attn_lambda_k1"])[0],
                           f(inp["attn_lambda_q2"])[0], f(inp["attn_lambda_k2"])[0]])
    small[:, 8:520] = lamv[None, :]
    Pm = np.zeros((32, 32), np.float32)
    for m in range(32):
        Pm[(m + 16) % 32, m] = 1.0
    small[0:32, 520:552] = Pm
    shared["small"] = small
    flat = {n: shared.pop(n).reshape(-1, 8192) for n, _ in WNAMES}
    for n, rows in WNAMES:
        assert flat[n].shape[0] == rows, (n, flat[n].shape)
    shards = [[] for _ in range(8)]
    for n, rows in WNAMES:
        for p in range(0, rows, 2048):
            pr = min(2048, rows - p)
            m = pr // 8
            for c in range(8):
                shards[c].append(flat[n][p + c * m: p + (c + 1) * m])
    shards = [np.concatenate(sh, axis=0) for sh in shards]
    in_maps = []
    for c in range(8):
        b, q = c // 4, c % 4
        order = [q] + [k for k in range(4) if k != q]
        idx = np.concatenate([np.arange(k * T, (k + 1) * T) for k in order])
        m = dict(shared)
        m["wsh"] = shards[c]
        m["xseq"] = np.ascontiguousarray(x[b][idx].T)
        m["pos"] = np.ascontiguousarray(pos[b][idx][None, :])
        m["memT"] = np.ascontiguousarray(mem[b].T)
        mk = np.zeros((128, 24), np.float32)
        mk[:, c] = 1.0
        if q > 0:
            mk[:, 8 + c - 1] = 1.0
        if q < 3:
            mk[:, 16 + c + 1] = 1.0
        m["masks"] = mk
        in_maps.append(m)
    return in_maps


_NC = {}


def kernel(**inputs):
    in_maps = prep_inputs(inputs)
    key = (tuple(DEBUG["dump"]), DEBUG["stop"])
    nc = build()
    res = run_bass_kernel_spmd(nc, in_maps, core_ids=list(range(8)))
    out = np.zeros((2, SEQ, D), np.float32)
    for c in range(8):
        b, q = c // 4, c % 4
        out[b, q * T:(q + 1) * T, :] = res.results[c]["out"].T
    DEBUG["last"] = res
    return out
```
